# Optimizing a Trainium2 kernel written in Bass

```python
import jax, jax.numpy as jnp
from jax import lax
import numpy as np

D_MODEL = 1024
BATCH = 8
SEQ = 2048
DEPTH = 2
DEC_BATCH = 128
DEC_SEQ = 8
PAST_LEN = 16384
PAGE_SIZE = 128

N_EVEN = (DEPTH + 1) // 2
N_ODD = DEPTH // 2
CHUNK = 128
H_A = 8
D_A = D_MODEL
HD_A = D_A // H_A
H_B = 8
D_B = D_MODEL
CONV_B_W = 31
D_C = D_MODEL
CONV_C_W = 3
D_FF = 2816
FFN_CONV_W = 3
EPS = 1e-6

kernel_name = 'hybrid_gmlp_conformer_shortconv_decoder_step'


def rmsnorm(x, g):
    xf = x.astype(jnp.float32)
    y = xf * lax.rsqrt(jnp.mean(xf * xf, axis=-1, keepdims=True) + EPS)
    return (y * g.astype(jnp.float32)).astype(x.dtype)


def layernorm(x, g, b):
    xf = x.astype(jnp.float32)
    mu = jnp.mean(xf, axis=-1, keepdims=True)
    var = jnp.mean(jnp.square(xf - mu), axis=-1, keepdims=True)
    y = (xf - mu) * lax.rsqrt(var + EPS) * g.astype(jnp.float32) + b.astype(jnp.float32)
    return y.astype(x.dtype)


def group_layernorm(x, g, b, groups):
    n, t, c = x.shape
    xf = x.astype(jnp.float32).reshape(n, t, groups, c // groups)
    mu = jnp.mean(xf, axis=-1, keepdims=True)
    var = jnp.mean(jnp.square(xf - mu), axis=-1, keepdims=True)
    y = ((xf - mu) * lax.rsqrt(var + EPS)).reshape(n, t, c)
    y = y * g.astype(jnp.float32) + b.astype(jnp.float32)
    return y.astype(x.dtype)


def causal_dwconv(x, prev, w, b=None):
    xp = jnp.concatenate([prev.astype(x.dtype), x], axis=1)
    y = lax.conv_general_dilated(
        xp, w[:, None, :].astype(x.dtype), window_strides=(1,), padding='VALID',
        dimension_numbers=('NWC', 'WIO', 'NWC'), feature_group_count=x.shape[-1])
    if b is not None:
        y = y + b.astype(x.dtype)
    return y, xp[:, xp.shape[1] - (w.shape[0] - 1):]


def adaln(c, w, b):
    m = (jax.nn.silu(c) @ w + b)[:, None, :]
    return jnp.split(m, 6, axis=-1)


def spatial_gate(v, ws, bs):
    n, t, _ = v.shape
    L = min(t, CHUNK)
    mask = jnp.tril(jnp.ones((L, L), dtype=bool))
    wm = jnp.where(mask, ws[:, :L, :L], 0).astype(v.dtype)
    vc = v.reshape(n, t // L, L, H_A, HD_A)
    out = jnp.einsum('hts,bnshd->bnthd', wm, vc)
    out = out + jnp.transpose(bs[:, :L]).astype(v.dtype)[None, None, :, :, None]
    return out.reshape(n, t, D_A)


def mixer_ab(h, prev_b, w_in, sgu_w, sgu_b, ln_g, ln_b, cw, cb, bn_g, bn_b, w_out):
    z = h @ w_in
    zu, zv, za, zg = jnp.split(z, [D_A, 2 * D_A, 2 * D_A + D_B], axis=-1)
    u = jax.nn.gelu(zu)
    v = layernorm(jax.nn.gelu(zv), ln_g, ln_b)
    a_out = u * spatial_gate(v, sgu_w, sgu_b)
    glu = za * jax.nn.sigmoid(zg)
    cv, new_b = causal_dwconv(glu, prev_b, cw, cb)
    b_out = jax.nn.silu(group_layernorm(cv, bn_g, bn_b, H_B))
    y = jnp.concatenate([a_out, b_out], axis=-1) @ w_out
    return y, new_b, v


def mixer_c(h, prev_c, w_in, cw, w_out):
    z = h @ w_in
    bg, cg, hx = jnp.split(z, 3, axis=-1)
    conv, new_c = causal_dwconv(cg * hx, prev_c, cw)
    return (bg * conv) @ w_out, new_c


def conv_ffn(h, prev_f, w_up, cw, cb, w_down):
    up = h @ w_up
    upc, new_f = causal_dwconv(up, prev_f, cw, cb)
    g, v = jnp.split(upc, 2, axis=-1)
    return (jax.nn.silu(g) * v) @ w_down, new_f


def trunk(x, c, prev_b, prev_c, prev_f, p):
    new_b, new_c, new_f, new_v = [], [], [], []
    for i in range(DEPTH):
        j = i // 2
        sh1, sc1, g1, sh2, sc2, g2 = adaln(c, p['ada_w'][i], p['ada_b'][i])
        h = rmsnorm(x, p['norm_mix_g'][i]) * (1 + sc1) + sh1
        if i % 2 == 0:
            y, nb, v = mixer_ab(h, prev_b[j], p['w_in_ab'][j], p['sgu_w'][j], p['sgu_b'][j],
                                p['sgu_ln_g'][j], p['sgu_ln_b'][j], p['convb_w'][j], p['convb_b'][j],
                                p['convb_ln_g'][j], p['convb_ln_b'][j], p['w_out_ab'][j])
            new_b.append(nb)
            new_v.append(v)
        else:
            y, nc = mixer_c(h, prev_c[j], p['w_in_c'][j], p['convc_w'][j], p['w_out_c'][j])
            new_c.append(nc)
        x = x + g1 * y
        h = rmsnorm(x, p['norm_ffn_g'][i]) * (1 + sc2) + sh2
        y, nf = conv_ffn(h, prev_f[i], p['ffn_up'][i], p['ffn_conv_w'][i], p['ffn_conv_b'][i],
                         p['ffn_down'][i])
        new_f.append(nf)
        x = x + g2 * y
    y = rmsnorm(x, p['final_g'])
    return y, jnp.stack(new_b), jnp.stack(new_c), jnp.stack(new_f), jnp.stack(new_v)


def setup_inputs(seed: int = 0) -> dict:
    key = jax.random.key(seed)
    k = jax.random.split(key, 32)
    f32 = jnp.float32

    def nrm(i, shape, s):
        return jax.random.normal(k[i], shape, f32) * s

    D = D_MODEL
    return {
        'x_prompt': nrm(0, (BATCH, SEQ, D), 1.0),
        'x_sample': nrm(1, (DEC_BATCH, DEC_SEQ, D), 1.0),
        'c_prompt': nrm(2, (BATCH, D), 1.0),
        'c_sample': nrm(3, (DEC_BATCH, D), 1.0),
        'state_conv_b': nrm(4, (N_EVEN, DEC_BATCH, CONV_B_W - 1, D_B), 0.5),
        'state_conv_c': nrm(5, (N_ODD, DEC_BATCH, CONV_C_W - 1, D_C), 1.0),
        'state_ffn': nrm(6, (DEPTH, DEC_BATCH, FFN_CONV_W - 1, 2 * D_FF), 1.0),
        'w_in_ab': nrm(7, (N_EVEN, D, 2 * D_A + 2 * D_B), D ** -0.5),
        'sgu_w': nrm(8, (N_EVEN, H_A, CHUNK, CHUNK), CHUNK ** -0.5),
        'sgu_b': 1.0 + nrm(9, (N_EVEN, H_A, CHUNK), 0.1),
        'sgu_ln_g': 1.0 + nrm(10, (N_EVEN, D_A), 0.05),
        'sgu_ln_b': nrm(11, (N_EVEN, D_A), 0.02),
        'convb_w': nrm(12, (N_EVEN, CONV_B_W, D_B), CONV_B_W ** -0.5),
        'convb_b': nrm(13, (N_EVEN, D_B), 0.02),
        'convb_ln_g': 1.0 + nrm(14, (N_EVEN, D_B), 0.05),
        'convb_ln_b': nrm(15, (N_EVEN, D_B), 0.02),
        'w_out_ab': nrm(16, (N_EVEN, D_A + D_B, D), (D_A + D_B) ** -0.5),
        'w_in_c': nrm(17, (N_ODD, D, 3 * D_C), D ** -0.5),
        'convc_w': nrm(18, (N_ODD, CONV_C_W, D_C), CONV_C_W ** -0.5),
        'w_out_c': nrm(19, (N_ODD, D_C, D), D_C ** -0.5),
        'norm_mix_g': 1.0 + nrm(20, (DEPTH, D), 0.05),
        'norm_ffn_g': 1.0 + nrm(21, (DEPTH, D), 0.05),
        'ada_w': nrm(22, (DEPTH, D, 6 * D), 0.5 * D ** -0.5),
        'ada_b': nrm(23, (DEPTH, 6 * D), 0.02),
        'ffn_up': nrm(24, (DEPTH, D, 2 * D_FF), D ** -0.5),
        'ffn_conv_w': nrm(25, (DEPTH, FFN_CONV_W, 2 * D_FF), FFN_CONV_W ** -0.5),
        'ffn_conv_b': nrm(26, (DEPTH, 2 * D_FF), 0.02),
        'ffn_down': nrm(27, (DEPTH, D_FF, D), D_FF ** -0.5),
        'final_g': 1.0 + nrm(28, (D,), 0.05),
    }


def reference(x_prompt, x_sample, c_prompt, c_sample, state_conv_b, state_conv_c, state_ffn,
              w_in_ab, sgu_w, sgu_b, sgu_ln_g, sgu_ln_b, convb_w, convb_b, convb_ln_g, convb_ln_b,
              w_out_ab, w_in_c, convc_w, w_out_c, norm_mix_g, norm_ffn_g, ada_w, ada_b,
              ffn_up, ffn_conv_w, ffn_conv_b, ffn_down, final_g):
    p = dict(w_in_ab=w_in_ab, sgu_w=sgu_w, sgu_b=sgu_b, sgu_ln_g=sgu_ln_g, sgu_ln_b=sgu_ln_b,
             convb_w=convb_w, convb_b=convb_b, convb_ln_g=convb_ln_g, convb_ln_b=convb_ln_b,
             w_out_ab=w_out_ab, w_in_c=w_in_c, convc_w=convc_w, w_out_c=w_out_c,
             norm_mix_g=norm_mix_g, norm_ffn_g=norm_ffn_g, ada_w=ada_w, ada_b=ada_b,
             ffn_up=ffn_up, ffn_conv_w=ffn_conv_w, ffn_conv_b=ffn_conv_b, ffn_down=ffn_down,
             final_g=final_g)
    nb = x_prompt.shape[0]
    dt = x_prompt.dtype
    zb = jnp.zeros((N_EVEN, nb, CONV_B_W - 1, D_B), dt)
    zc = jnp.zeros((N_ODD, nb, CONV_C_W - 1, D_C), dt)
    zf = jnp.zeros((DEPTH, nb, FFN_CONV_W - 1, 2 * D_FF), dt)
    y_prompt, pb, pc, pf, _ = trunk(x_prompt, c_prompt, zb, zc, zf, p)
    y_sample, sb, sc, sf, sv = trunk(x_sample, c_sample, state_conv_b, state_conv_c, state_ffn, p)
    return (y_prompt, y_sample, pb, pc, pf, sb, sc, sf, sv)
```

```python
import contextlib
import math
import numpy as np
import concourse.bass as bass
import concourse.mybir as mybir
from concourse.bass_utils import run_bass_kernel_spmd

F32 = mybir.dt.float32
BF16 = mybir.dt.bfloat16
I32 = mybir.dt.int32
AF = mybir.ActivationFunctionType
ALU = mybir.AluOpType

D = 1024
DFF = 2816
NCH = 8
NFC = 22
EPS = 1e-6
SEQ = 2048
NS = 16
TS_ = 8
SBW = 1152
GC = 2.0 * math.sqrt(2.0 / math.pi)
SQ044 = math.sqrt(0.044715)


class Tok:
    __slots__ = ("key", "sem", "val")

    def __init__(self, key, sem, val):
        self.key = key
        self.sem = sem
        self.val = val


class TS:
    def __init__(self, *toks):
        self.d = {}
        self.add(*toks)

    def add(self, *toks):
        for t in toks:
            if t is None:
                continue
            if isinstance(t, TS):
                self.add(*t.d.values())
            elif isinstance(t, (list, tuple)):
                self.add(*t)
            else:
                c = self.d.get(t.key)
                if c is None or c.val < t.val:
                    self.d[t.key] = t
        return self

    def toks(self):
        return list(self.d.values())


class EngQ:
    def __init__(self, prog, name):
        self.prog = prog
        self.name = name
        self.ops = []
        self.cnt = 0
        self.sem = None
        self.seen = {}
        self.tags = []

    def _waits(self, deps, out):
        for t in deps:
            if t is None:
                continue
            if isinstance(t, TS):
                self._waits(t.toks(), out)
            elif isinstance(t, (list, tuple)):
                self._waits(t, out)
            else:
                if self.seen.get(t.key, 0) >= t.val:
                    continue
                self.seen[t.key] = t.val
                out.append(t)
        return out

    def add(self, fn, deps=(), mark=True):
        waits = self._waits(deps, [])
        tok = None
        if mark:
            self.cnt += 1
            tok = Tok(self.name, None, self.cnt)
        self.ops.append((waits, fn, mark, None))
        self.tags.append(self.prog.tag)
        return tok

    def dma(self, slot, out, in_, deps=(), **kw):
        waits = self._waits(deps, [])
        slot.cnt += 16
        tok = Tok(slot.key, slot, slot.cnt)
        self.ops.append((waits, lambda e: e.dma_start(out=out, in_=in_, **kw), False, slot))
        self.tags.append(self.prog.tag)
        return tok

    def emit(self, e):
        prog = self.prog
        for waits, fn, mark, slot in self.ops:
            for t in waits:
                sem = t.sem.sem if t.sem is not None else prog.q[t.key].sem
                e.wait_ge(sem, t.val)
            ins = fn(e)
            if slot is not None:
                ins.then_inc(slot.sem, 16)
            elif mark:
                ins.then_inc(self.sem, 1)


class DmaSlot:
    def __init__(self, key):
        self.key = key
        self.sem = None
        self.cnt = 0


class Prog:
    def __init__(self):
        self.nc = bass.Bass("TRN2", target_bir_lowering=False)
        self.q = {n: EngQ(self, n) for n in ("pe", "act", "dve", "pool", "sp")}
        self.slots = []
        self.stack = contextlib.ExitStack()
        self.tag = 'setup'

    def slot(self, name):
        s = DmaSlot("dma_" + name + "_" + str(len(self.slots)))
        self.slots.append(s)
        return s

    def sb(self, name, shape, dt):
        return self.stack.enter_context(self.nc.sbuf_tensor(name, list(shape), dt))

    def ps(self, name, shape, dt=F32):
        return self.stack.enter_context(self.nc.psum_tensor(name, list(shape), dt))

    def dram(self, name, shape, dt, kind):
        return self.nc.dram_tensor(name, list(shape), dt, kind=kind).ap()

    def finish(self):
        nc = self.nc
        for q in self.q.values():
            q.sem = self.stack.enter_context(nc.semaphore("s_" + q.name))
        for s in self.slots:
            s.sem = self.stack.enter_context(nc.semaphore(s.key))
        with nc.Block() as block:
            @block.tensor
            def _(e):
                self.q["pe"].emit(e)

            @block.scalar
            def _(e):
                self.q["act"].emit(e)

            @block.vector
            def _(e):
                self.q["dve"].emit(e)

            @block.gpsimd
            def _(e):
                self.q["pool"].emit(e)

            @block.sync
            def _(e):
                self.q["sp"].emit(e)
        self.stack.close()
        return nc


def op(q, method, *args, deps=(), mark=True, **kw):
    return q.add(lambda e: getattr(e, method)(*args, **kw), deps=deps, mark=mark)


class Res:
    def __init__(self, ap):
        self.ap = ap
        self.free = TS()


class Rot:
    def __init__(self, items):
        self.items = items
        self.all = list(items)
        self.i = 0

    def reserve(self, n):
        self.items = self.all[:len(self.all) - n]
        self.i = self.i % len(self.items)
        return self.all[len(self.all) - n:]

    def unreserve(self):
        self.items = list(self.all)

    def get(self):
        r = self.items[self.i]
        self.i = (self.i + 1) % len(self.items)
        return r


_NC_CACHE = {}


def build(debug=()):
    P = Prog()
    nc = P.nc
    pe, act, dve, pool, sp = (P.q[n] for n in ("pe", "act", "dve", "pool", "sp"))

    def din(name, shape):
        return P.dram(name, shape, F32, "ExternalInput")

    def dout(name, shape):
        return P.dram(name, shape, F32, "ExternalOutput")

    xp = din("xp", [SEQ, D]); xs = din("xs", [128, D]); c17 = din("c17", [17, D])
    st_b = din("st_b", [512, D])
    st_c = din("st_c", [32, D]); st_f = din("st_f", [2, 32, 2 * DFF])
    st_b_raw = din("st_b_raw", [NS, 30, D])
    w_in_ab = din("w_in_ab", [D, 4 * D]); sgu_w = din("sgu_w", [8, 128, 128]); sgu_b = din("sgu_b", [8, 128])
    ln_gb = din("ln_gb", [2, D])
    w_out_ab = din("w_out_ab", [2 * D, D]); w_in_c = din("w_in_c", [D, 3 * D]); w_out_c = din("w_out_c", [D, D])
    ada_w = din("ada_w", [2, D, 6 * D]); ffn_up = din("ffn_up", [2, D, 2 * DFF]); ffn_down = din("ffn_down", [2, DFF, D])
    V1 = din("V1", [42, D]); V2 = din("V2", [8, 2 * DFF]); V3 = din("V3", [2, 6 * D])
    ident_d = din("ident", [128, 128]); mask_d = din("mask", [128, 128])

    y_p = dout("y_p", [SEQ, D]); y_s = dout("y_s", [128, D])
    nb_p = dout("nb_p", [30, D]); nc_p = dout("nc_p", [2, D]); nf_p = dout("nf_p", [2, 2, 2 * DFF])
    nb_s = dout("nb_s", [NS, 30, D]); nc_s = dout("nc_s", [32, D]); nf_s = dout("nf_s", [2, 32, 2 * DFF])
    nv_s = dout("nv_s", [128, D])

    ARENA = 212800
    arena = P.sb("arena", [128, ARENA // 4], F32)[:]
    cur = [0]

    def alloc(nbytes):
        off = (cur[0] + 63) // 64 * 64
        cur[0] = off + nbytes
        assert cur[0] <= ARENA, ("SBUF overflow", cur[0], ARENA)
        return off

    def view(off, shape, dt):
        esz = 2 if dt == BF16 else 4
        nel = int(np.prod(shape[1:]))
        nw = (nel * esz + 3) // 4
        ap = arena[0:shape[0], off // 4: off // 4 + nw]
        if dt != F32:
            ap = ap.bitcast(dt)
            if esz == 2:
                ap = ap[:, 0:nel]
        if len(shape) == 3:
            ap = ap.rearrange("p (a b) -> p a b", a=shape[1])
        elif len(shape) == 4:
            ap = ap.rearrange("p (a b c) -> p a b c", a=shape[1], b=shape[2])
        return ap

    def new(shape, dt):
        esz = 2 if dt == BF16 else 4
        return view(alloc(int(np.prod(shape[1:])) * esz), shape, dt)

    ident_f = new([128, 128], F32); mask = new([128, 128], F32)
    ones_bf = new([128, 128], BF16); ones128 = new([128, 128], BF16)
    colv1 = new([128, NCH, 42], F32); colv2 = new([128, 44, 8], F32)
    modT = new([128, 2, 48, 17], F32)
    WT_p = new([128, 8, 128], BF16); WT_s = new([128, 8, 128], BF16)
    sel16 = new([16, 8, 128], BF16); bias16 = new([16, 2, 128], BF16)
    lnbc = new([128, 2, D], F32)
    glu_carry = new([128, NCH, 30], BF16); c_carry = new([128, NCH, 2], F32)
    f_carry = new([128, 2, 44, 2], F32)
    sfT = new([128, 2, 44, 32], F32); scT = new([128, NCH, 32], F32)
    tailb_p = new([128, NCH, 32], F32); tailb_s = new([128, NCH, 128], F32)
    epsc = new([128, 1], F32)
    xu_tmp = new([128, 128], F32)
    adabT = new([128, 48, 2], F32)
    cT = new([128, 8, 128], BF16)
    NSLOT = 4
    SLOTB = 8192
    ring_off = [alloc(SLOTB) for _ in range(NSLOT)]
    XT0 = (cur[0] + 63) // 64 * 64
    xT = new([128, NCH, SBW], F32)
    hT = new([128, NCH, SBW], BF16)
    STAGE0 = cur[0]

    banks = Rot([Res(P.ps("ps%d" % i, [128, 512])[:]) for i in range(8)])

    def slot(name):
        return P.slot(name)

    s_const = slot("const")
    s_outmisc = slot("outmisc")
    out_toks = TS()

    class Ring:
        def __init__(self):
            self.slots = [slot("ring%d" % i) for i in range(NSLOT)]
            self.free = [TS() for _ in range(NSLOT)]
            self.sched = []
            self.tok = {}
            self.nxt = 0
            self.issued = 0

        def plan(self, name, dmas):
            self.sched.append((name, dmas))

        def _issue(self, i):
            name, dmas = self.sched[i]
            s = i % NSLOT
            tok = None
            for dst, src in dmas(ring_off[s]):
                tok = pool.dma(self.slots[s], dst, src, deps=[self.free[s]])
            self.tok[i] = tok

        def start(self):
            self.busy = [False] * NSLOT
            self._pump()

        def _pump(self):
            while self.issued < len(self.sched) and not self.busy[self.issued % NSLOT]:
                self.busy[self.issued % NSLOT] = True
                self._issue(self.issued)
                self.issued += 1

        def get(self, name):
            i = self.nxt
            self.nxt += 1
            assert self.sched[i][0] == name, (self.sched[i][0], name)
            assert i < self.issued, "ring: piece not issued (held too many)"
            return i, ring_off[i % NSLOT], self.tok[i]

        def release(self, i, tok):
            self.free[i % NSLOT] = TS(tok)
            self.busy[i % NSLOT] = False
            self._pump()

    ring = Ring()

    def wsrc(w2d, c0, n):
        return w2d.rearrange("(k p) n -> p k n", p=128)[:, :, c0:c0 + n]

    def plan_all():
        for nb in range(12):
            ring.plan("ada", lambda off, nb=nb: [(view(off, [128, 8, 512], BF16), wsrc(ada_w[0], nb * 512, 512))])
        ada1 = [0]

        def plan_ada1():
            nb = ada1[0]
            if nb < 12:
                ring.plan("ada", lambda off, nb=nb: [(view(off, [128, 8, 512], BF16), wsrc(ada_w[1], nb * 512, 512))])
                ada1[0] += 1
        for sbi in range(2):
            ring.plan("U", lambda off: [(view(off, [128, 8, 512], BF16), wsrc(w_in_ab, 0, 512))])
            for hh in range(2):
                ring.plan("V", lambda off, hh=hh: [(view(off, [128, 8, 512], BF16), wsrc(w_in_ab, D + hh * 512, 512))])
            ring.plan("U", lambda off: [(view(off, [128, 8, 512], BF16), wsrc(w_in_ab, 512, 512))])
            for p in range(4):
                ring.plan("AG", lambda off, p=p: [
                    (view(off, [128, 8, 2, 256], BF16)[:, :, 0, :], wsrc(w_in_ab, 2 * D + p * 256, 256)),
                    (view(off, [128, 8, 2, 256], BF16)[:, :, 1, :], wsrc(w_in_ab, 3 * D + p * 256, 256))])
            if sbi == 0:
                for _ in range(12):
                    plan_ada1()
            for p in range(4):
                ring.plan("WO", lambda off, p=p: [(view(off, [128, 16, 256], BF16), wsrc(w_out_ab, p * 256, 256))])
            for L in range(2):
                if L == 1:
                    for j in range(8):
                        ring.plan("C", lambda off, j=j: [
                            (view(off, [128, 8, 3, 128], BF16)[:, :, i, :], wsrc(w_in_c, i * D + j * 128, 128)) for i in range(3)])
                    for p in range(2):
                        ring.plan("WC", lambda off, p=p: [(view(off, [128, 8, 512], BF16), wsrc(w_out_c, p * 512, 512))])
                for half in range(2):
                    k0, nk = (0, 12) if half == 0 else (12, 10)
                    for p in range(nk // 2):
                        ring.plan("UP", lambda off, L=L, c=k0 + 2 * p: [
                            (view(off, [128, 8, 2, 256], BF16)[:, :, 0, :], wsrc(ffn_up[L], c * 128, 256)),
                            (view(off, [128, 8, 2, 256], BF16)[:, :, 1, :], wsrc(ffn_up[L], DFF + c * 128, 256))])
                    if half == 1:
                        for (m0, cnt) in ((0, 3), (3, 3), (6, 2)):
                            ring.plan("DN", lambda off, L=L, m0=m0, cnt=cnt, k0=k0, nk=nk: [
                                (view(off, [128, nk, cnt * 128], BF16),
                                 ffn_down[L].rearrange("(k p) n -> p k n", p=128)[:, k0:k0 + nk, m0 * 128:(m0 + cnt) * 128])])
                    else:
                        for m in range(8):
                            ring.plan("DN", lambda off, L=L, m=m, k0=k0, nk=nk: [
                                (view(off, [128, nk, 128], BF16),
                                 ffn_down[L].rearrange("(k p) n -> p k n", p=128)[:, k0:k0 + nk, m * 128:(m + 1) * 128])])

    plan_all()
    ring.start()

    def newton(a, y, t, iters, deps, q=None):
        q = q or dve
        tk = op(dve, "tensor_scalar", out=y.bitcast(I32), in0=a.bitcast(I32), scalar1=-0.5, scalar2=1597463007.0,
                op0=ALU.mult, op1=ALU.add, deps=deps)
        for _ in range(iters):
            tk = op(dve, "tensor_tensor", out=t, in0=a, in1=y, op=ALU.mult, deps=[tk])
            tk = op(dve, "scalar_tensor_tensor", out=t, in0=t, scalar=-0.5, in1=y, op0=ALU.mult, op1=ALU.mult, deps=[tk])
            tk = op(dve, "scalar_tensor_tensor", out=y, in0=t, scalar=1.5, in1=y, op0=ALU.add, op1=ALU.mult, deps=[tk])
        return tk

    def q3(ap2d):
        return ap2d.rearrange("p (q t) -> p q t", q=NS)

    def view_of(t3, j0, shape):
        return t3[0:shape[0], j0:j0 + 4, :].rearrange("p a b -> p (a b)")

    def bc_s(col17):
        return col17[:, 1:17].unsqueeze(2).to_broadcast([128, NS, TS_])

    t_id = sp.dma(s_const, ident_f, ident_d)
    t_mk = sp.dma(s_const, mask, mask_d)
    t_ln = sp.dma(s_const, lnbc[:, 0, :], ln_gb[0:1, :].to_broadcast([128, D]))
    t_ln = sp.dma(s_const, lnbc[:, 1, :], ln_gb[1:2, :].to_broadcast([128, D]))
    t_const = TS(t_ln)
    t_ones = op(dve, "memset", ones_bf, 1.0)
    t_ones = op(dve, "memset", ones128, 1.0 / 128)
    t_eps = op(dve, "memset", epsc, EPS)
    scr = [STAGE0]

    def snew(shape, dt):
        esz = 2 if dt == BF16 else 4
        nb = int(np.prod(shape[1:])) * esz
        off = (scr[0] + 63) // 64 * 64
        scr[0] = off + nb
        assert scr[0] <= ARENA, "setup scratch overflow"
        return view(off, shape, dt)

    xT_off_scr = [None]

    s_set = slot("setup")
    bias8 = snew([8, 2, 128], F32); bhi = snew([8, 2, 128], BF16); blo = snew([8, 2, 128], BF16); bhf = snew([8, 2, 128], F32)
    s_b8 = slot("b8")
    t_b8 = sp.dma(s_b8, bias8[:, 0, :], sgu_b)
    with nc.allow_non_contiguous_dma(reason="tiny bias broadcast"):
        t_b8 = sp.dma(s_b8, bias8[:, 1, :].rearrange("p (q t) -> p q t", q=NS),
                      sgu_b[:, 0:8].unsqueeze(1).to_broadcast([8, NS, 8]))
    t_sel = op(dve, "tensor_copy", out=sel16[0:8], in_=ident_f[0:8, 0:8].unsqueeze(2).to_broadcast([8, 8, 128]), deps=[t_const])
    t1 = op(dve, "tensor_copy", out=bhi, in_=bias8, deps=[t_b8])
    t1 = op(dve, "tensor_copy", out=bhf, in_=bhi, deps=[t1])
    t1 = op(dve, "tensor_tensor", out=blo, in0=bias8, in1=bhf, op=ALU.subtract, deps=[t1])
    s_sel = slot("sel")
    t2 = act.dma(s_sel, bias16[0:8], bhi, deps=[t1])
    t2 = act.dma(s_sel, bias16[8:16], blo, deps=[t1])
    t2 = act.dma(s_sel, sel16[8:16], sel16[0:8], deps=[t_sel])
    t_sel = TS(t2)

    v1s = snew([42, D], F32); big22 = snew([32, 2 * DFF], F32); v2s = big22[0:8, :]
    c17s = snew([17, D], F32); csig = snew([17, D], F32)
    mtok = [Res(view_of(tailb_s, 0, [17, 512])), Res(view_of(tailb_s, 4, [17, 512]))]
    sguw = snew([128, 8, 128], F32); sguw2 = snew([128, 8, 128], F32)
    v3s = xT[0:2, :, :].rearrange("p a b -> p (a b)")[:, 0:6 * D]
    stcs = snew([32, D], F32)
    s_big = slot("big")
    s_set2 = slot("setup2")
    t_v1 = sp.dma(s_set, v1s, V1)
    t_c = sp.dma(s_set, c17s, c17)
    t_sw = sp.dma(s_set, sguw, sgu_w.rearrange("h t s -> t h s"))
    t_v2 = sp.dma(s_big, v2s, V2)
    t_v3 = sp.dma(s_set2, v3s, V3)
    t_set = TS(t_sw)
    t_set2 = TS(t_v3)
    t_z = op(dve, "memset", sguw2, 0.0)
    s_set3 = slot("setup3")
    t_bd = None
    with nc.allow_non_contiguous_dma(reason="8x8 blocks"):
        for qq in range(NS):
            t_bd = act.dma(s_set3, sguw2[qq * 8:(qq + 1) * 8, :, qq * 8:(qq + 1) * 8],
                          sgu_w[:, 0:8, 0:8].rearrange("h t s -> t h s"), deps=[t_z])

    def transposes_to(bank, srcs, width, deps):
        tk = None
        for i, s in enumerate(srcs):
            rows = s.shape[0]
            tk = op(pe, "transpose", bank.ap[:, i * width:i * width + rows], s, ident_f[0:rows, 0:rows],
                    deps=[deps, bank.free, t_const] if i == 0 else (), mark=(i == len(srcs) - 1))
        return tk

    bk = banks.get()
    tk = transposes_to(bk, [v1s[:, k * 128:(k + 1) * 128] for k in range(8)], 42, [t_set])
    t_colv1 = op(dve, "tensor_copy", out=colv1, in_=bk.ap[:, 0:336].rearrange("p (a b) -> p a b", a=8), deps=[tk])
    bk.free = TS(t_colv1)
    bk = banks.get()
    tk = transposes_to(bk, [v2s[:, k * 128:(k + 1) * 128] for k in range(44)], 8, [t_v2])
    big_free = TS(tk)
    t_colv2 = op(dve, "tensor_copy", out=colv2, in_=bk.ap[:, 0:352].rearrange("p (a b) -> p a b", a=44), deps=[tk])
    bk.free = TS(t_colv2)
    bk = banks.get()
    tk = transposes_to(bk, [v3s[:, k * 128:(k + 1) * 128] for k in range(48)], 2, [t_set2])
    t_adab = op(dve, "tensor_copy", out=adabT, in_=bk.ap[:, 0:96].rearrange("p (a b) -> p a b", a=48), deps=[tk])
    bk.free = TS(t_adab)
    t1 = op(act, "activation", out=csig, in_=c17s, func=AF.Sigmoid, deps=[t_set])
    t2 = op(dve, "tensor_tensor", out=csig, in0=csig, in1=c17s, op=ALU.mult, deps=[t1])
    bk = banks.get()
    tk = transposes_to(bk, [csig[:, k * 128:(k + 1) * 128] for k in range(8)], 17, [t2])
    t_cz = op(dve, "memset", cT, 0.0)
    t_cT = op(dve, "tensor_copy", out=cT[:, :, 0:17], in_=bk.ap[:, 0:136].rearrange("p (a b) -> p a b", a=8), deps=[tk, t_cz])
    bk.free = TS(t_cT)
    def ada_step(L, nb, mt, defer=False):
        wi, woff, wtok = ring.get("ada")
        W = view(woff, [128, 8, 512], BF16)
        bk = banks.get()
        for k in range(8):
            tk = op(pe, "matmul", bk.ap[:, :], lhsT=cT[:, k, :], rhs=W[:, k, :], start=(k == 0), stop=(k == 7),
                    deps=[wtok, t_cT, bk.free] if k == 0 else (), mark=(k == 7))
        ring.release(wi, tk)
        te = op(act, "activation", out=mt.ap, in_=bk.ap[0:17, :], func=AF.Copy, deps=[tk, mt.free])
        bk.free = TS(te)

        def part2():
            bk2 = banks.get()
            tk2 = transposes_to(bk2, [mt.ap[:, i * 128:(i + 1) * 128] for i in range(4)], 17, [te])
            mt.free = TS(tk2)
            tm = op(dve, "tensor_tensor", out=modT[:, L, nb * 4:(nb + 1) * 4, :],
                    in0=bk2.ap[:, 0:68].rearrange("p (a b) -> p a b", a=4),
                    in1=adabT[:, nb * 4:(nb + 1) * 4, L:L + 1].to_broadcast([128, 4, 17]), op=ALU.add, deps=[tk2, t_adab])
            bk2.free = TS(tm)
            return tm
        if defer:
            return part2
        return part2()

    def ada_finish(L, t_mod):
        for (lo, vi) in ((8, L), (32, 2 + L)):
            t_mod = op(dve, "scalar_tensor_tensor", out=modT[:, L, lo:lo + 8, :], in0=modT[:, L, lo:lo + 8, :], scalar=1.0,
                       in1=colv1[:, :, vi:vi + 1].to_broadcast([128, 8, 17]), op0=ALU.add, op1=ALU.mult, deps=[t_mod, t_colv1])
        return t_mod

    ada0_early_tok = None
    for nb_ in range(4):
        ada0_early_tok = ada_step(0, nb_, mtok[nb_ % 2])
    for (src, dst, dep) in ((sguw, WT_p, t_set), (sguw2, WT_s, TS(t_bd))):
        for g in range(2):
            bk = banks.get()
            tk = transposes_to(bk, [src[:, g * 4 + i, :] for i in range(4)], 128, [dep])
            tw = op(dve, "tensor_tensor", out=dst[:, g * 4:(g + 1) * 4, :], in0=bk.ap.rearrange("p (a b) -> p a b", a=4),
                    in1=mask.unsqueeze(1).to_broadcast([128, 4, 128]), op=ALU.mult, deps=[tk])
            bk.free = TS(tw)
    t_WT = tw
    t_idc = op(dve, "tensor_scalar", out=mask, in0=ident_f, scalar1=-1.0 / 128, scalar2=None, op0=ALU.add, deps=[t_WT, t_const])
    identc = mask
    cb_bf = snew([128, NCH], BF16)
    t1 = op(dve, "tensor_copy", out=cb_bf, in_=colv1[:, :, 36], deps=[t_colv1])
    bk = banks.get()
    tk = op(pe, "matmul", bk.ap[:, 0:NCH], lhsT=ones128, rhs=cb_bf, start=True, stop=True, deps=[t1, bk.free, t_ones])
    t_cbc = op(dve, "tensor_tensor", out=colv1[:, :, 36], in0=colv1[:, :, 36], in1=bk.ap[:, 0:NCH], op=ALU.subtract, deps=[tk])
    bk.free = TS(t_cbc)
    LATE0 = (ARENA - (32 * 1024)) // 64 * 64
    late_f = view(LATE0, [32, 2 * DFF], F32)
    late_c = view(LATE0 + 2 * DFF * 4 + 64, [32, D], F32)
    s_late = slot("late")
    s_late2 = slot("late2")

    def do_states(dep):
        t_c_ = sp.dma(s_late2, late_c, st_c, deps=[dep])
        free_ = TS(dep)
        t_sf = None
        for L in range(2):
            t_ld = sp.dma(s_late, late_f, st_f[L], deps=[free_])
            for g in range(3):
                ks = list(range(g * 16, min(44, g * 16 + 16)))
                bk = banks.get()
                tk = transposes_to(bk, [late_f[:, k * 128:(k + 1) * 128] for k in ks], 32, [t_ld])
                t_sf = op(act, "activation", out=sfT[:, L, ks[0]:ks[-1] + 1, :],
                          in_=bk.ap[:, 0:len(ks) * 32].rearrange("p (a b) -> p a b", a=len(ks)), func=AF.Copy, deps=[tk])
                bk.free = TS(t_sf)
            free_ = TS(tk)
        bk = banks.get()
        tk = transposes_to(bk, [late_c[:, k * 128:(k + 1) * 128] for k in range(8)], 32, [t_c_])
        t_sc = op(act, "activation", out=scT, in_=bk.ap[:, 0:256].rearrange("p (a b) -> p a b", a=8), func=AF.Copy, deps=[tk])
        bk.free = TS(t_sc)
        return t_sf, t_sc

    t_sfT = None
    t_scT = None

    def do_ada0():
        t_mod = None
        for nb in range(12):
            t_mod = ada_step(0, nb, mtok[nb % 2])
        return ada_finish(0, t_mod)
    setup_done = TS(t_idc, t_cbc, t_adab, t_cT, t_WT, t_colv2, t_sel, t_ones, t_eps, Tok("pe", None, pe.cnt), Tok("act", None, act.cnt))

    MOD = dict(sh1=0, gm1=8, g1=16, sh2=24, gm2=32, g2=40)

    def modk(L, name, k):
        return modT[:, L, MOD[name] + k, :]

    s_x = [slot("x%d" % i) for i in range(4)]
    s_y = [slot("y%d" % i) for i in range(4)]
    s_nv = slot("nv")
    s_stb = slot("stb")
    s_stb_b = slot("stb_b")

    state = dict(x_ready={}, h_rd=TS(setup_done), stage_free=TS(setup_done))

    def blocks_of(sbi):
        bl = [dict(kind="p", c0=0, n=512, idx=0), dict(kind="p", c0=512, n=512, idx=1)]
        if sbi == 1:
            bl.append(dict(kind="s", c0=1024, n=128, idx=2))
        return bl

    def bv(ap2d, blk):
        return ap2d if blk["kind"] == "p" else q3(ap2d)

    for sbi in range(2):
        blocks = blocks_of(sbi)
        ntile = 8 + (1 if sbi == 1 else 0)
        x_ready = {}
        stage = [STAGE0]

        def tnew(shape, dt):
            esz = 2 if dt == BF16 else 4
            nb = int(np.prod(shape[1:])) * esz
            off = (stage[0] + 63) // 64 * 64
            stage[0] = off + nb
            assert stage[0] <= ARENA, ("stage overflow", stage[0], ARENA)
            return view(off, shape, dt)

        def norm_gen(blk, L, which, final, pre, T, sfree, h_free):
            c0, n, bi = blk["c0"], blk["n"], blk["idx"]
            sq, av, yv, tv, tks = T["sq"], T["av"], T["yv"], T["tv"], T["tks"]
            if pre is not None:
                bk = pre.banks[bi]; tk = pre.tok[bi]
            else:
                tsq = op(act, "activation", out=sq.ap[:, :, 0:n], in_=xT[:, :, c0:c0 + n], func=AF.Square,
                         deps=[x_ready[bi], sq.free, sfree])
                bk = banks.get()
                for k in range(8):
                    tk = op(pe, "matmul", bk.ap[:, 0:n], lhsT=ones_bf, rhs=sq.ap[:, k, 0:n], start=(k == 0), stop=(k == 7),
                            deps=[tsq, bk.free] if k == 0 else (), mark=(k == 7))
                sq.free = TS(tk)
            a = av[bi % len(av)]; y = yv[bi % len(yv)]
            aa = a.ap[:, 0:n]; yy = y.ap[:, 0:n]; tt_ = tv[:, 0:n]
            ta = op(act, "activation", out=aa, in_=bk.ap[:, 0:n], func=AF.Identity, bias=epsc[:, 0:1], scale=1.0 / D,
                    deps=[tk, a.free, sfree, t_eps])
            bk.free = TS(ta)
            ty = op(dve, "tensor_scalar", out=yy.bitcast(I32), in0=aa.bitcast(I32), scalar1=-0.5, scalar2=1597463007.0,
                    op0=ALU.mult, op1=ALU.add, deps=[ta, y.free])
            yield
            for _ in range(2):
                ty = op(dve, "tensor_tensor", out=tt_, in0=aa, in1=yy, op=ALU.mult, deps=[ty])
                ty = op(dve, "scalar_tensor_tensor", out=tt_, in0=tt_, scalar=-0.5, in1=yy, op0=ALU.mult, op1=ALU.mult, deps=[ty])
                ty = op(dve, "scalar_tensor_tensor", out=yy, in0=tt_, scalar=1.5, in1=yy, op0=ALU.add, op1=ALU.mult, deps=[ty])
                yield
            a.free = TS(ty)
            if final:
                return (y, ty)
            hdone = TS()
            if blk["kind"] == "s":
                gb = MOD["gm%d" % which]; sb_ = MOD["sh%d" % which]
                for hf in range(2):
                    t = tks[hf]
                    t3 = t.ap[:, 0:512].rearrange("p (a b) -> p a b", a=4)
                    t4 = t.ap[:, 0:512].rearrange("p (a q t) -> p a q t", a=4, q=NS)
                    xv = xT[:, 4 * hf:4 * hf + 4, c0:c0 + n]
                    t1 = op(dve, "tensor_tensor", out=t3, in0=xv, in1=y.ap[:, 0:n].unsqueeze(1).to_broadcast([128, 4, n]),
                            op=ALU.mult, deps=[ty, t.free])
                    t1 = op(dve, "tensor_tensor", out=t4, in0=t4,
                            in1=modT[:, L, gb + 4 * hf:gb + 4 * hf + 4, 1:17].unsqueeze(3).to_broadcast([128, 4, NS, TS_]),
                            op=ALU.mult, deps=[t1])
                    t2 = op(dve, "tensor_tensor", out=hT[:, 4 * hf:4 * hf + 4, c0:c0 + n].rearrange("p a (q t) -> p a q t", q=NS),
                            in0=t4, in1=modT[:, L, sb_ + 4 * hf:sb_ + 4 * hf + 4, 1:17].unsqueeze(3).to_broadcast([128, 4, NS, TS_]),
                            op=ALU.add, deps=[t1, h_free])
                    t.free = TS(t2)
                    hdone.add(t2)
                    if hf == 0:
                        yield
                y.free = TS(hdone)
                return hdone
            for k in range(8):
                xv = xT[:, k, c0:c0 + n]
                t = tks[k % 3]
                if blk["kind"] == "p":
                    t1 = op(dve, "scalar_tensor_tensor", out=t.ap[:, 0:n], in0=xv, scalar=modk(L, "gm%d" % which, k)[:, 0:1],
                            in1=y.ap[:, 0:n], op0=ALU.mult, op1=ALU.mult, deps=[ty, t.free])
                    t2 = op(act, "activation", out=hT[:, k, c0:c0 + n], in_=t.ap[:, 0:n], func=AF.Identity,
                            bias=modk(L, "sh%d" % which, k)[:, 0:1], scale=1.0, deps=[t1, h_free])
                    t.free = TS(t2)
                    hdone.add(t2)
                else:
                    t1 = op(dve, "tensor_tensor", out=t.ap[:, 0:n], in0=xv, in1=y.ap[:, 0:n], op=ALU.mult, deps=[ty, t.free])
                    t1 = op(dve, "tensor_tensor", out=q3(t.ap[:, 0:n]), in0=q3(t.ap[:, 0:n]),
                            in1=bc_s(modk(L, "gm%d" % which, k)), op=ALU.mult, deps=[t1])
                    t2 = op(dve, "tensor_tensor", out=q3(hT[:, k, c0:c0 + n]), in0=q3(t.ap[:, 0:n]),
                            in1=bc_s(modk(L, "sh%d" % which, k)), op=ALU.add, deps=[t1, h_free])
                    t.free = TS(t2)
                    hdone.add(t2)
                if k < 7:
                    yield
            y.free = TS(hdone)
            return hdone

        def norm_block(*args):
            g = norm_gen(*args)
            while True:
                try:
                    next(g)
                except StopIteration as e_:
                    return e_.value

        class Stepper:
            def __init__(self, g, dst, key):
                self.g, self.dst, self.key, self.done = g, dst, key, False

            def step(self, nunits):
                for _ in range(nunits):
                    if self.done:
                        return
                    try:
                        next(self.g)
                    except StopIteration as e_:
                        self.dst[self.key] = e_.value
                        self.done = True

            def finish(self):
                while not self.done:
                    self.step(1)

        P.tag = 'A_loadx'
        xst = [Res(tnew([128, D], F32)) for _ in range(4)]
        prev_free = TS(state["stage_free"])
        T0 = dict(sq=Res(tnew([128, NCH, 512], BF16)), av=[Res(tnew([128, 512], F32)) for _ in range(2)],
                  yv=[Res(tnew([128, 512], F32)) for _ in range(2)], tv=tnew([128, 512], F32),
                  tks=[Res(tnew([128, 512], F32)) for _ in range(3)])
        hw_first = {}
        first_steppers = []
        units_left = {}
        xw = TS()
        ada0_nb = [4]
        ada0_tok = [ada0_early_tok]

        def ada0_some(k_):
            for _ in range(k_):
                if ada0_nb[0] < 12:
                    ada0_tok[0] = ada_step(0, ada0_nb[0], mtok[ada0_nb[0] % 2])
                    ada0_nb[0] += 1

        for ti in range(ntile):
            if sbi == 0:
                ada0_some(1)
            src = xp[sbi * 1024 + ti * 128: sbi * 1024 + (ti + 1) * 128, :] if ti < 8 else xs
            st = xst[ti % 4]
            tl = sp.dma(s_x[ti % 4], st.ap, src, deps=[st.free, prev_free])
            for g in range(2):
                bk = banks.get()
                tk = None
                for j in range(4):
                    k = g * 4 + j
                    tk = op(pe, "transpose", bk.ap[:, j * 128:(j + 1) * 128], st.ap[:, k * 128:(k + 1) * 128], ident_f,
                            deps=[tl, bk.free, setup_done] if j == 0 else (), mark=(j == 3))
                dst = xT[:, g * 4:(g + 1) * 4, ti * 128:(ti + 1) * 128]
                srcv = bk.ap.rearrange("p (a b) -> p a b", a=4)
                if g == 0:
                    te = op(act, "activation", out=dst, in_=srcv, func=AF.Copy, deps=[tk, prev_free])
                else:
                    te = op(dve, "tensor_copy", out=dst, in_=srcv, deps=[tk, prev_free])
                bk.free = TS(te)
                xw.add(te)
                if g == 1:
                    st.free = TS(tk)
            for stp_ in first_steppers:
                k_ = min(3, units_left[id(stp_)])
                stp_.step(k_)
                units_left[id(stp_)] -= k_
            blk_done = next((b for b in blocks if b["c0"] + b["n"] == (ti + 1) * 128), None)
            if blk_done is not None:
                x_ready[blk_done["idx"]] = TS(xw)
                stp_ = Stepper(norm_gen(blk_done, 0, 1, False, None, T0, prev_free, state["h_rd"]), hw_first, blk_done["idx"])
                first_steppers.append(stp_)
                units_left[id(stp_)] = 3 if sbi == 0 else 99
        if sbi == 0:
            for stp_ in first_steppers:
                stp_.step(units_left[id(stp_)])
            P.tag = 'ada0'
            ada0_some(12)
            t_mod0 = ada_finish(0, ada0_tok[0])
            t_sfT, t_scT = do_states(TS(setup_done, prev_free))
        P.tag = 'rmsnorm'
        for stp_ in first_steppers:
            stp_.finish()
        state["hw_pre"] = hw_first
        state["stage_free"] = TS(Tok("pe", None, pe.cnt), Tok("act", None, act.cnt), Tok("dve", None, dve.cnt))

        def rmsnorm(L, which, final=False):
            hw_pre = state.pop("hw_pre", None)
            if hw_pre is not None:
                state.pop("pre", None)
                banks.unreserve()
                P.tag = 'rmsnorm'
                stage[0] = STAGE0
                state["stage_free"] = TS(Tok("pe", None, pe.cnt), Tok("dve", None, dve.cnt), Tok("act", None, act.cnt))
                return hw_pre
            pre = state.pop("pre", None)
            if pre is not None:
                pre.flush()
            P.tag = 'rmsnorm'
            stage[0] = STAGE0
            T = dict(sq=Res(tnew([128, NCH, 512], BF16)), av=[Res(tnew([128, 512], F32)) for _ in range(2)],
                     yv=[Res(tnew([128, 512], F32)) for _ in range(3 if final else 2)], tv=tnew([128, 512], F32),
                     tks=[Res(tnew([128, 512], F32)) for _ in range(3)])
            sfree = TS(state["stage_free"])
            hw = {}
            for blk in blocks:
                hw[blk["idx"]] = norm_block(blk, L, which, final, pre, T, sfree, state["h_rd"])
            if pre is not None:
                banks.unreserve()
            state["stage_free"] = TS(Tok("pe", None, pe.cnt), Tok("dve", None, dve.cnt), Tok("act", None, act.cnt))
            return hw

        def early_T(tiles):
            return dict(sq=None, av=[Res(tiles[0])], yv=[Res(tiles[1])], tv=tiles[2], tks=[Res(tiles[3]), Res(tiles[4]), Res(tiles[5])])

        def gelu(ps_ap, out_ap, n, deps, out_free=None):
            return op(act, "activation", out=out_ap, in_=ps_ap, func=AF.Gelu_apprx_tanh, deps=[deps, out_free])

        def x_update(ps_ap, m, blk, L, gname, deps):
            c0, n = blk["c0"], blk["n"]
            xv = xT[:, m, c0:c0 + n]
            g = modk(L, gname, m)
            if blk["kind"] == "p":
                return op(dve, "scalar_tensor_tensor", out=xv, in0=ps_ap, scalar=g[:, 0:1], in1=xv, op0=ALU.mult, op1=ALU.add,
                          deps=deps)
            t1 = op(dve, "tensor_tensor", out=q3(xu_tmp[:, 0:n]), in0=q3(ps_ap), in1=bc_s(g), op=ALU.mult, deps=deps)
            return op(dve, "tensor_tensor", out=xv, in0=xv, in1=xu_tmp[:, 0:n], op=ALU.add, deps=[t1])

        class StatAcc:
            def __init__(self, sqr):
                self.sqr = sqr
                rb = banks.reserve(len(blocks))
                self.banks = {blk["idx"]: rb[i] for i, blk in enumerate(blocks)}
                self.pend = None
                self.tok = {}

            def push(self, m, blk, tu):
                c0, n = blk["c0"], blk["n"]
                sq = self.sqr.get()
                e = op(act, "activation", out=sq.ap[:, 0:n], in_=xT[:, m, c0:c0 + n], func=AF.Square, deps=[tu, sq.free])
                self.flush()
                self.pend = (m, blk, sq, e)

            def flush(self):
                if self.pend is None:
                    return
                m, blk, sq, e = self.pend
                n, bi = blk["n"], blk["idx"]
                bk = self.banks[bi]
                tk = op(pe, "matmul", bk.ap[:, 0:n], lhsT=ones_bf, rhs=sq.ap[:, 0:n], start=(m == 0), stop=(m == 7),
                        deps=[e, bk.free, t_ones] if m == 0 else [e])
                sq.free = TS(tk)
                self.tok[bi] = tk
                self.pend = None

        hw = rmsnorm(0, 1)
        stage[0] = STAGE0
        uT = tnew([128, NCH, SBW], BF16)
        R1 = stage[0]
        vB = tnew([128, 9, D], BF16)
        stage[0] = R1
        gluP = tnew([128, NCH, 30 + 1024], BF16)
        gluS = tnew([128, NCH, NS, 38], BF16)
        R1end = stage[0]
        diags = [Res(tnew([128, 31, 128], BF16)) for _ in range(2)]
        TMP0 = stage[0]
        vraw = [Res(tnew([128, D], F32)) for _ in range(2)]
        lns = [Res((tnew([128, 2, 6], F32), tnew([128, 2], F32), tnew([128, 1], F32), tnew([128, 1], F32), tnew([128, 1], F32),
                    tnew([128, 1], F32))) for _ in range(2)]
        sfree = TS(state["stage_free"])

        P.tag = 'U'
        u_done = {}
        wiU0, woffU0, wtokU0 = ring.get("U")
        wi0, woff0, wtok0 = ring.get("V")
        wi1, woff1, wtok1 = ring.get("V")
        wiU1, woffU1, wtokU1 = ring.get("U")
        u_list = []

        def mk_u(p, jj, blk, W, wtok, last, wi):
            def f():
                j = p * 4 + jj
                c0, n, bi = blk["c0"], blk["n"], blk["idx"]
                bk = banks.get()
                for k in range(8):
                    tk = op(pe, "matmul", bk.ap[:, 0:n], lhsT=W[:, k, jj * 128:(jj + 1) * 128], rhs=hT[:, k, c0:c0 + n],
                            start=(k == 0), stop=(k == 7), deps=[wtok, hw[bi], bk.free] if k == 0 else (), mark=(k == 7))
                tg = gelu(bk.ap[:, 0:n], uT[:, j, c0:c0 + n], n, [tk, sfree])
                bk.free = TS(tg)
                u_done[bi] = TS(tg)
                if last:
                    ring.release(wi, tk)
            return f

        for p, (wi_, woff_, wtok_) in enumerate(((wiU0, woffU0, wtokU0), (wiU1, woffU1, wtokU1))):
            W = view(woff_, [128, 8, 512], BF16)
            pairs = [(jj, blk) for blk in blocks for jj in range(4)] if p == 0 else [(jj, blk) for jj in range(4) for blk in blocks]
            for i_, (jj, blk) in enumerate(pairs):
                u_list.append(mk_u(p, jj, blk, W, wtok_, i_ == len(pairs) - 1, wi_))

        P.tag = 'V'
        Wv = [view(woff0, [128, 8, 512], BF16), view(woff1, [128, 8, 512], BF16)]
        v_done = {}
        vst = {}

        def v_ph1(ti):
            bi = min(ti // 4, 2)
            vr = vraw[ti % 2]
            st_ = lns[ti % 2]
            tg = None
            for hh in range(2):
                bk = banks.get()
                for k in range(8):
                    tk = op(pe, "matmul", bk.ap, lhsT=hT[:, k, ti * 128:(ti + 1) * 128], rhs=Wv[hh][:, k, :], start=(k == 0),
                            stop=(k == 7), deps=[wtok0, wtok1, hw[bi], bk.free] if k == 0 else (), mark=(k == 7))
                tg = gelu(bk.ap, vr.ap[:, hh * 512:(hh + 1) * 512], 512, [tk, sfree], out_free=vr.free)
                bk.free = TS(tg)
            lnst, lnmv, lna, lny, lnt, nmr = st_.ap
            t1 = op(dve, "bn_stats", out=lnst[:, 0, :], in_=vr.ap[:, 0:512], deps=[tg, st_.free])
            t1 = op(dve, "bn_stats", out=lnst[:, 1, :], in_=vr.ap[:, 512:1024], deps=[tg])
            t1 = op(dve, "bn_aggr", out=lnmv, in_=lnst.rearrange("p a b -> p (a b)"), deps=[t1])
            t1 = op(dve, "tensor_scalar", out=lna, in0=lnmv[:, 1:2], scalar1=EPS, scalar2=None, op0=ALU.add, deps=[t1])
            t1 = newton(lna, lny, lnt, 2, [t1])
            t1 = op(dve, "scalar_tensor_tensor", out=nmr, in0=lnmv[:, 0:1], scalar=-1.0, in1=lny, op0=ALU.mult, op1=ALU.mult, deps=[t1])
            vst[ti] = (vr, st_, t1, tk)

        def v_ph2(ti):
            vr, st_, t1, tk = vst.pop(ti)
            lnst, lnmv, lna, lny, lnt, nmr = st_.ap
            e1 = op(act, "activation", out=vr.ap, in_=vr.ap, func=AF.Identity, bias=nmr, scale=lny, deps=[t1])
            st_.free = TS(e1)
            t1 = op(dve, "tensor_tensor", out=vr.ap, in0=vr.ap, in1=lnbc[:, 0, :], op=ALU.mult, deps=[e1, t_const])
            if ti < 8:
                t2 = op(dve, "tensor_tensor", out=vB[:, ti, :], in0=vr.ap, in1=lnbc[:, 1, :], op=ALU.add, deps=[t1, sfree])
                vr.free = TS(t2)
            else:
                t1 = op(dve, "tensor_tensor", out=vr.ap, in0=vr.ap, in1=lnbc[:, 1, :], op=ALU.add, deps=[t1])
                t2 = op(act, "activation", out=vB[:, ti, :], in_=vr.ap, func=AF.Copy, deps=[t1, sfree])
                t3 = sp.dma(s_nv, nv_s, vr.ap, deps=[t1])
                out_toks.add(t3)
                vr.free = TS(t2, t3)
            v_done[ti] = t2

        n_u = len(u_list)
        per = -(-n_u // ntile)
        ui = 0
        for step in range(ntile + 1):
            if step < ntile:
                v_ph1(step)
            if step >= 1:
                v_ph2(step - 1)
            for _ in range(per):
                if ui < n_u:
                    u_list[ui]()
                    ui += 1
        while ui < n_u:
            u_list[ui]()
            ui += 1
        tk = Tok("pe", None, pe.cnt)
        ring.release(wi0, tk)
        ring.release(wi1, tk)

        P.tag = 'SGU'
        a_done = {}
        for blk in blocks:
            c0, n, bi = blk["c0"], blk["n"], blk["idx"]
            tiles = [c0 // 128 + i for i in range(n // 128)]
            WT = WT_p if blk["kind"] == "p" else WT_s
            bvar = 0 if blk["kind"] == "p" else 1
            for h in range(8):
                bk = banks.get()
                for i, ti in enumerate(tiles):
                    op(pe, "matmul", bk.ap[:, i * 128:(i + 1) * 128], lhsT=vB[:, ti, h * 128:(h + 1) * 128], rhs=WT[:, h, :],
                       start=True, stop=False, deps=[v_done[ti], bk.free, t_WT] if i == 0 else [v_done[ti]], mark=False)
                    tk = op(pe, "matmul", bk.ap[:, i * 128:(i + 1) * 128], lhsT=sel16[:, h, :], rhs=bias16[:, bvar, :],
                            start=False, stop=True, deps=[t_sel], mark=(i == len(tiles) - 1))
                ta = op(dve, "tensor_tensor", out=uT[:, h, c0:c0 + n], in0=uT[:, h, c0:c0 + n], in1=bk.ap[:, 0:n], op=ALU.mult,
                        deps=[tk, u_done[bi]])
                bk.free = TS(ta)
                a_done[bi] = TS(ta)
        v_dead = TS(Tok("pe", None, pe.cnt))

        P.tag = 'Bhist'
        stage[0] = TMP0
        CVF0 = (stage[0] + 63) // 64 * 64
        cvf = Rot([Res(tnew([128, 512], F32)) for _ in range(3)])
        CVB0 = (stage[0] + 63) // 64 * 64
        cvb = Rot([Res(tnew([128, 512], BF16)) for _ in range(2)])
        sqb = Rot([Res(tnew([128, 512], BF16)) for _ in range(2)])
        t1_t = tnew([128, 512], F32); a_t = tnew([128, 512], F32); y_t = tnew([128, 512], F32); nt_t = t1_t
        sgt = Rot([Res(t1_t), Res(a_t)])
        stbs = Res(view(CVF0, [128, D], F32))
        bfree = TS(v_dead, Tok("dve", None, dve.cnt), Tok("act", None, act.cnt))
        hist_jobs = []
        if sbi == 0:
            t_hist = op(dve, "memset", gluP[:, :, 0:30], 0.0, deps=[bfree])
        else:
            t_hist = op(dve, "tensor_copy", out=gluP[:, :, 0:30], in_=glu_carry, deps=[bfree])
            stb2 = [stbs, Res(view(CVB0, [128, D], F32))]
            s_stb2 = [s_stb, s_stb_b]
            hist_dma = {}

            def hist_load(rt):
                if rt < 4 and rt not in hist_dma:
                    b_ = stb2[rt % 2]
                    hist_dma[rt] = sp.dma(s_stb2[rt % 2], b_.ap, st_b[rt * 128:(rt + 1) * 128, :], deps=[b_.free, bfree])

            def mk_hist(rt):
                def f():
                    hist_load(rt)
                    hist_load(rt + 1)
                    b_ = stb2[rt % 2]
                    tl = hist_dma[rt]
                    for g in range(2):
                        bk = banks.get()
                        tk = transposes_to(bk, [b_.ap[:, (g * 4 + i) * 128:(g * 4 + i + 1) * 128] for i in range(4)], 128, [tl])
                        th = op(act, "activation", out=gluS[:, g * 4:(g + 1) * 4, rt * 4:(rt + 1) * 4, 0:30],
                                in_=bk.ap.rearrange("p (a b c) -> p a b c", a=4, b=4)[:, :, :, 0:30], func=AF.Copy,
                                deps=[tk, bfree])
                        bk.free = TS(th)
                        glu_done.add(th)
                    b_.free = TS(tk)
                return f
            hist_load(0)
            hist_load(1)
            hist_jobs = [mk_hist(rt) for rt in range(4)]
            out_toks.add(sp.dma(s_outmisc, nb_s[:, 0:22, :], st_b_raw[:, 8:30, :]))

        P.tag = 'AG'
        glu_done = TS(t_hist)
        for p in range(4):
            wi, woff, wtok = ring.get("AG")
            W = view(woff, [128, 8, 2, 256], BF16)
            if hist_jobs:
                hist_jobs.pop(0)()
            for jj in range(2):
                j = p * 2 + jj
                for blk in blocks:
                    c0, n, bi = blk["c0"], blk["n"], blk["idx"]
                    bka = banks.get(); bkg = banks.get()
                    for (bk, ci) in ((bka, 0), (bkg, 1)):
                        for k in range(8):
                            tk = op(pe, "matmul", bk.ap[:, 0:n], lhsT=W[:, k, ci, jj * 128:(jj + 1) * 128], rhs=hT[:, k, c0:c0 + n],
                                    start=(k == 0), stop=(k == 7), deps=[wtok, hw[bi], bk.free] if k == 0 else (), mark=(k == 7))
                    sg = sgt.get()
                    t1 = op(act, "activation", out=sg.ap[:, 0:n], in_=bkg.ap[:, 0:n], func=AF.Sigmoid, deps=[tk, sg.free, bfree])
                    bkg.free = TS(t1)
                    if blk["kind"] == "p":
                        t2 = op(dve, "tensor_tensor", out=gluP[:, j, 30 + c0:30 + c0 + n], in0=bka.ap[:, 0:n], in1=sg.ap[:, 0:n],
                                op=ALU.mult, deps=[t1, t_hist, bfree])
                        if sbi == 1 and bi == 1:
                            t2 = op(dve, "tensor_tensor", out=tailb_p[:, j, 0:30], in0=bka.ap[:, n - 30:n], in1=sg.ap[:, n - 30:n],
                                    op=ALU.mult, deps=[t1])
                    else:
                        t2 = op(dve, "tensor_tensor", out=tailb_s[:, j, :], in0=bka.ap[:, 0:n], in1=sg.ap[:, 0:n], op=ALU.mult,
                                deps=[t1])
                        t2 = op(dve, "tensor_copy", out=gluS[:, j, :, 30:38], in_=q3(tailb_s[:, j, :]), deps=[t2, t_hist, bfree])
                    sg.free = TS(t2)
                    bka.free = TS(t2)
                    glu_done.add(t2)
            ring.release(wi, tk)
        h_dead = TS(Tok("pe", None, pe.cnt))
        bT = hT

        P.tag = 'conv'
        b_done = {}
        b_tok = {}
        its = [(j, blk) for j in range(8) for blk in blocks]
        stA = {}; stB = {}
        a_free = [TS()]
        tdg_of = {}

        def build_diag(j_):
            if j_ >= 8 or j_ in tdg_of:
                return
            tdg_of[j_] = op(pool, "tensor_tensor", out=diags[j_ % 2].ap, in0=identc.unsqueeze(1).to_broadcast([128, 31, 128]),
                            in1=colv1[:, j_, 5:36].unsqueeze(2).to_broadcast([128, 31, 128]), op=ALU.mult,
                            deps=[diags[j_ % 2].free, t_colv1, t_const, sfree])

        def phA(i):
            j, blk = its[i]
            c0, n, bi = blk["c0"], blk["n"], blk["idx"]
            if blk is blocks[0]:
                build_diag(j)
                build_diag(j + 1)
            dg = diags[j % 2]
            bk = banks.get()
            for k in range(31):
                rhs = gluP[:, j, c0 + k:c0 + k + n] if blk["kind"] == "p" else gluS[:, j, :, k:k + 8]
                tk = op(pe, "matmul", bk.ap[:, 0:n], lhsT=dg.ap[:, k, :], rhs=rhs, start=(k == 0), stop=(k == 30),
                        deps=[tdg_of[j], glu_done, bk.free] if k == 0 else (), mark=(k == 30))
            if blk is blocks[-1]:
                dg.free = TS(tk)
            cb = colv1[:, j, 36:37]
            cf = cvf.get(); sq = sqb.get()
            e1 = op(act, "activation", out=cf.ap[:, 0:n], in_=bk.ap[:, 0:n], func=AF.Identity, bias=cb, scale=1.0,
                    deps=[tk, cf.free, bfree])
            e3 = op(act, "activation", out=sq.ap[:, 0:n], in_=bk.ap[:, 0:n], func=AF.Square, bias=cb, scale=1.0, deps=[sq.free])
            bk.free = TS(e3)
            stA[i] = (cf, sq, e1, e3)

        def phB(i):
            j, blk = its[i]
            c0, n, bi = blk["c0"], blk["n"], blk["idx"]
            cf, sq, e1, e3 = stA.pop(i)
            bk2 = banks.get()
            m2 = op(pe, "matmul", bk2.ap[:, 0:n], lhsT=ones128, rhs=sq.ap[:, 0:n], start=True, stop=True, deps=[e3, bk2.free, t_ones])
            sq.free = TS(m2)
            d3 = op(act, "activation", out=a_t[:, 0:n], in_=bk2.ap[:, 0:n], func=AF.Identity, bias=epsc[:, 0:1], scale=1.0,
                    deps=[m2, a_free[0], bfree])
            bk2.free = TS(d3)
            ty = newton(a_t[:, 0:n], y_t[:, 0:n], nt_t[:, 0:n], 2, [d3])
            d6 = op(dve, "tensor_tensor", out=cf.ap[:, 0:n], in0=cf.ap[:, 0:n], in1=y_t[:, 0:n], op=ALU.mult, deps=[ty, e1])
            a_free[0] = TS(ty)
            stB[i] = (cf, d6)

        def phC(i):
            j, blk = its[i]
            c0, n, bi = blk["c0"], blk["n"], blk["idx"]
            cf, d6 = stB.pop(i)
            g_ = colv1[:, j, 37:38]; b_ = colv1[:, j, 38:39]
            e5 = op(act, "activation", out=bT[:, j, c0:c0 + n], in_=cf.ap[:, 0:n], func=AF.Silu, bias=b_, scale=g_, deps=[d6, h_dead])
            cf.free = TS(e5)
            b_done[bi] = TS(e5)
            b_tok[(j, bi)] = e5

        mtok1 = [Res(view_of(tailb_s, 0, [17, 512])), Res(view_of(tailb_s, 4, [17, 512]))] if sbi == 0 else None
        n_ada1 = 0
        ada_p2 = []
        for step in range(len(its) + 2):
            if step < len(its):
                phA(step)
            if 0 <= step - 1 < len(its):
                phB(step - 1)
            if 0 <= step - 2 < len(its):
                phC(step - 2)
            if sbi == 0 and step >= 2:
                if ada_p2:
                    state["t_mod1"] = ada_p2.pop()()
                    if n_ada1 == 12:
                        state["t_mod1"] = ada_finish(1, state["t_mod1"])
                if n_ada1 < 12:
                    ada_p2.append(ada_step(1, n_ada1, mtok1[n_ada1 % 2], defer=True))
                    n_ada1 += 1
        assert not (sbi == 0 and (ada_p2 or n_ada1 != 12))
        if sbi == 0:
            op(act, "activation", out=glu_carry, in_=gluP[:, :, 1024:1054], func=AF.Copy, deps=[glu_done])

        P.tag = 'WO'
        sacc = StatAcc(Rot([Res(view(CVB0 + 1024 * i, [128, 512], BF16)) for i in range(3)]))
        for r_ in sacc.sqr.items:
            r_.free = TS(Tok("pe", None, pe.cnt), Tok("act", None, act.cnt))
        state["pre"] = sacc
        def wo_mm(W, wtok, mm, blk, bk, klo, khi):
            c0, n, bi = blk["c0"], blk["n"], blk["idx"]
            tk_ = None
            for k in range(klo, khi):
                rhs = uT[:, k, c0:c0 + n] if k < 8 else bT[:, k - 8, c0:c0 + n]
                tk_ = op(pe, "matmul", bk.ap[:, 0:n], lhsT=W[:, k, mm * 128:(mm + 1) * 128], rhs=rhs, start=(k == 0),
                         stop=(k == 15), deps=[wtok, a_done[bi], bk.free] if k == 0 else ([b_tok[(k - 8, bi)]] if k >= 8 else ()),
                         mark=(k == 15))
            return tk_

        def wo_fin(bk, m, blk, tk_):
            n, bi = blk["n"], blk["idx"]
            tu = x_update(bk.ap[:, 0:n], m, blk, 0, "g1", [tk_])
            bk.free = TS(tu)
            x_ready[bi] = TS(tu)
            sacc.push(m, blk, tu)

        def wo_group(W, wtok, m, mm, blk):
            bk = banks.get()
            tk_ = wo_mm(W, wtok, mm, blk, bk, 0, 16)
            wo_fin(bk, m, blk, tk_)
            return tk_

        for p in range(2):
            wi, woff, wtok = ring.get("WO")
            W = view(woff, [128, 16, 256], BF16)
            for mm in range(2):
                if p == 0 and mm == 0:
                    ob = {blk["idx"]: banks.get() for blk in blocks}
                    for blk in blocks:
                        wo_mm(W, wtok, mm, blk, ob[blk["idx"]], 0, 14)
                    for blk in blocks:
                        tk = wo_mm(W, wtok, mm, blk, ob[blk["idx"]], 14, 16)
                        wo_fin(ob[blk["idx"]], 0, blk, tk)
                    continue
                for blk in blocks:
                    tk = wo_group(W, wtok, p * 2 + mm, mm, blk)
            ring.release(wi, tk)
        wo2 = [ring.get("WO") for _ in range(2)]
        Wo2 = [view(w_[1], [128, 16, 256], BF16) for w_ in wo2]
        nT = early_T([cvf.items[0].ap, cvf.items[1].ap, cvf.items[2].ap, t1_t, a_t, y_t])
        n_sfree = TS(Tok("pe", None, pe.cnt), Tok("dve", None, dve.cnt), Tok("act", None, act.cnt))
        hw_e = {}
        pend_blk = None
        pend_hfree = None
        stp = None
        for blk in blocks:
            for m in range(4, 8):
                p2, mm = divmod(m - 4, 2)
                tk = wo_group(Wo2[p2], wo2[p2][2], m, mm, blk)
                if m == 4 and pend_blk is not None:
                    stp = Stepper(norm_gen(pend_blk, 0, 2, False, sacc, nT, n_sfree, pend_hfree), hw_e, pend_blk["idx"])
                    pend_blk = None
                if stp is not None:
                    stp.step(4)
            if stp is not None:
                stp.finish()
                stp = None
            pend_blk = blk
            pend_hfree = TS(tk)
        sacc.flush()
        hw_e[pend_blk["idx"]] = norm_block(pend_blk, 0, 2, False, sacc, nT, n_sfree, pend_hfree)
        for w_ in wo2:
            ring.release(w_[0], tk)
        state["hw_pre"] = hw_e
        state["h_rd"] = TS(Tok("pe", None, pe.cnt))
        state["stage_free"] = TS(Tok("pe", None, pe.cnt), Tok("dve", None, dve.cnt), Tok("act", None, act.cnt))

        def ffn(L):
            hw = rmsnorm(L, 2)
            P.tag = 'UP'
            stage[0] = STAGE0
            actT = tnew([128, 12, SBW], BF16)
            upP = [[Res(tnew([128, 2 + 1024], F32)) for _ in range(2)] for _ in range(2)]
            upS = [[Res(tnew([128, NS, 10], F32)) for _ in range(2)] for _ in range(2)]
            accs = Rot([(Res(tnew([128, 512], F32)), Res(tnew([128, 512], F32))) for _ in range(3)])
            sgf = Rot([Res(tnew([128, 512], F32)) for _ in range(2)])
            sqr_f = Rot([Res(tnew([128, 512], BF16)) for _ in range(3)])
            sfree = TS(state["stage_free"])
            rot = 0
            pend_gate = []
            flat_c = [k0 + 2 * p + jj for (k0, nk) in ((0, 12), (12, 10)) for p in range(nk // 2) for jj in range(2)]
            hist_toks = {}

            def init_hist(i):
                if i >= len(flat_c) or i in hist_toks:
                    return
                c_ = flat_c[i]
                r_ = i % 2
                toks = []
                for gv in range(2):
                    cj = gv * NFC + c_
                    u_ = upP[gv][r_]; us_ = upS[gv][r_]
                    if sbi == 0:
                        th = op(pool, "memset", u_.ap[:, 0:2], 0.0, deps=[u_.free, sfree])
                    else:
                        th = op(pool, "tensor_copy", out=u_.ap[:, 0:2], in_=f_carry[:, L, cj, :], deps=[u_.free, sfree])
                        th = op(pool, "tensor_copy", out=us_.ap[:, :, 0:2],
                                in_=sfT[:, L, cj, :].rearrange("p (q r) -> p q r", q=NS), deps=[us_.free, sfree, t_sfT])
                    toks.append(th)
                hist_toks[i] = toks

            for half in range(2):
                k0, nk = (0, 12) if half == 0 else (12, 10)
                act_done = {}
                act_tok = {}
                act_free = TS(Tok("pe", None, pe.cnt)) if half == 1 else sfree
                P.tag = 'UP'
                for p in range(nk // 2):
                    wi, woff, wtok = ring.get("UP")
                    W = view(woff, [128, 8, 2, 256], BF16)
                    chs = []
                    for jj in range(2):
                        c = k0 + 2 * p + jj
                        cl = 2 * p + jj
                        r = rot % 2
                        rot += 1
                        ups = [upP[0][r], upP[1][r]]
                        upss = [upS[0][r], upS[1][r]]
                        init_hist(rot - 1)
                        hist = hist_toks[rot - 1]
                        chs.append((jj, c, cl, ups, upss, hist, rot - 1))
                    first_piece = (half == 0 and p == 0)
                    pairs = [(ch, blk) for blk in blocks for ch in chs] if first_piece else [(ch, blk) for ch in chs for blk in blocks]
                    for (jj, c, cl, ups, upss, hist, cidx), blk in pairs:
                        c0, n, bi = blk["c0"], blk["n"], blk["idx"]
                        ac = accs.get()
                        tacc = []
                        for gv in range(2):
                            cj = gv * NFC + c
                            bk = banks.get()
                            for k in range(8):
                                tk = op(pe, "matmul", bk.ap[:, 0:n], lhsT=W[:, k, gv, jj * 128:(jj + 1) * 128],
                                        rhs=hT[:, k, c0:c0 + n], start=(k == 0), stop=(k == 7),
                                        deps=[wtok, hw[bi], bk.free] if k == 0 else (), mark=(k == 7))
                            w0 = colv2[:, cj, L * 3 + 0:L * 3 + 1]; w1 = colv2[:, cj, L * 3 + 1:L * 3 + 2]
                            w2 = colv2[:, cj, L * 3 + 2:L * 3 + 3]; cb = colv2[:, cj, 6 + L:7 + L]
                            if blk["kind"] == "p":
                                raw = ups[gv].ap
                                cur_v = raw[:, 2 + c0:2 + c0 + n]; m1 = raw[:, 1 + c0:1 + c0 + n]; m2 = raw[:, c0:c0 + n]
                                psv = bk.ap[:, 0:n]; accv = ac[gv].ap[:, 0:n]
                            else:
                                raw = upss[gv].ap
                                cur_v = raw[:, :, 2:10]; m1 = raw[:, :, 1:9]; m2 = raw[:, :, 0:8]
                                psv = q3(bk.ap[:, 0:n]); accv = q3(ac[gv].ap[:, 0:n])
                            e1 = op(act, "activation", out=cur_v, in_=psv, func=AF.Copy, deps=[tk, hist[gv], t_colv2, sfree])
                            e2 = op(act, "activation", out=accv, in_=psv, func=AF.Identity, bias=cb, scale=w2, deps=[ac[gv].free])
                            bk.free = TS(e2)
                            d1 = op(dve, "scalar_tensor_tensor", out=accv, in0=m1, scalar=w1, in1=accv, op0=ALU.mult, op1=ALU.add,
                                    deps=[e1, e2])
                            d2 = op(dve, "scalar_tensor_tensor", out=accv, in0=m2, scalar=w0, in1=accv, op0=ALU.mult, op1=ALU.add,
                                    deps=[d1])
                            tacc.append(d2)
                            if blk["kind"] == "p" and bi == 1:
                                ups[gv].free = TS(d2)
                                ups[gv].free = TS(op(pool, "tensor_copy", out=f_carry[:, L, cj, :], in_=raw[:, 1024:1026], deps=[d2]))
                            if blk["kind"] == "s":
                                upss[gv].free = TS(op(pool, "tensor_copy", out=sfT[:, L, cj, :].rearrange("p (q r) -> p q r", q=NS),
                                                      in_=raw[:, :, 8:10], deps=[d2]))
                        def gate(ac=ac, tacc=tacc, cl=cl, c0=c0, n=n, bi=bi, act_free=act_free, act_done=act_done, act_tok=act_tok):
                            sg = sgf.get()
                            e3 = op(act, "activation", out=sg.ap[:, 0:n], in_=ac[0].ap[:, 0:n], func=AF.Silu, deps=[tacc[0], sg.free])
                            d4 = op(dve, "tensor_tensor", out=actT[:, cl, c0:c0 + n], in0=sg.ap[:, 0:n], in1=ac[1].ap[:, 0:n], op=ALU.mult,
                                    deps=[e3, tacc[1], act_free])
                            sg.free = TS(d4); ac[0].free = TS(d4); ac[1].free = TS(d4)
                            act_done[bi] = TS(d4)
                            act_tok[(cl, bi)] = d4
                        if pend_gate:
                            pend_gate.pop()()
                        pend_gate.append(gate)
                        if blk is blocks[-1]:
                            init_hist(cidx + 2)
                    ring.release(wi, tk)
                if pend_gate:
                    pend_gate.pop()()
                P.tag = 'DN'
                if half == 1:
                    sacc = StatAcc(sqr_f)
                    state["pre"] = sacc
                if half == 1:
                    nL, nW, nF = (1, 1, False) if L == 0 else (0, 1, True)
                    dn = []
                    for (m0, cnt) in ((0, 3), (3, 3), (6, 2)):
                        wi_, woff_, wtok_ = ring.get("DN")
                        dn.append((wi_, view(woff_, [128, nk, cnt * 128], BF16), wtok_, m0, cnt))
                    tl_ = [upP[0][1].ap[:, 0:512], upP[0][1].ap[:, 512:1024], upP[1][0].ap[:, 0:512], upP[1][0].ap[:, 512:1024],
                           upP[1][1].ap[:, 0:512], upP[1][1].ap[:, 512:1024]]
                    if nF:
                        nT = dict(sq=None, av=[Res(tl_[0])], yv=[Res(tl_[1]), Res(tl_[2]), Res(tl_[3])], tv=tl_[4], tks=None)
                    else:
                        nT = early_T(tl_)
                    n_sfree = TS(Tok("dve", None, dve.cnt), Tok("act", None, act.cnt), Tok("pool", None, pool.cnt))
                    n_hfree = TS(Tok("pe", None, pe.cnt))
                    hw_e = {}
                    pend_blk = None
                    stp = None
                    def dn_mm(blk, m, bk, klo, khi):
                        c0, n, bi = blk["c0"], blk["n"], blk["idx"]
                        wi_, Wd, wtok_, m0, cnt = next(d_ for d_ in dn if d_[3] <= m < d_[3] + d_[4])
                        tk_ = None
                        for k in range(klo, khi):
                            tk_ = op(pe, "matmul", bk.ap[:, 0:n], lhsT=Wd[:, k, (m - m0) * 128:(m - m0 + 1) * 128],
                                     rhs=actT[:, k, c0:c0 + n], start=(k == 0), stop=(k == nk - 1),
                                     deps=[wtok_, bk.free, act_tok[(k, bi)]] if k == 0 else [act_tok[(k, bi)]], mark=(k == nk - 1))
                        return tk_

                    for blk in blocks:
                        c0, n, bi = blk["c0"], blk["n"], blk["idx"]
                        opened = {}
                        if blk is blocks[0]:
                            for m in range(3):
                                opened[m] = banks.get()
                                dn_mm(blk, m, opened[m], 0, nk - 3)
                        for m in range(8):
                            if m in opened:
                                bk = opened[m]
                                tk = dn_mm(blk, m, bk, nk - 3, nk)
                            else:
                                bk = banks.get()
                                tk = dn_mm(blk, m, bk, 0, nk)
                            tu = x_update(bk.ap[:, 0:n], m, blk, L, "g2", [tk])
                            bk.free = TS(tu)
                            x_ready[bi] = TS(tu)
                            sacc.push(m, blk, tu)
                            if m == 0 and pend_blk is not None:
                                stp = Stepper(norm_gen(pend_blk, nL, nW, nF, sacc, nT, n_sfree, n_hfree), hw_e, pend_blk["idx"])
                                pend_blk = None
                            if stp is not None:
                                stp.step(2)
                        if stp is not None:
                            stp.finish()
                            stp = None
                        pend_blk = blk
                    sacc.flush()
                    hw_e[pend_blk["idx"]] = norm_block(pend_blk, nL, nW, nF, sacc, nT, n_sfree, n_hfree)
                    for d_ in dn:
                        ring.release(d_[0], tk)
                    state["hw_pre"] = hw_e
                else:
                    for m in range(8):
                        wi, woff, wtok = ring.get("DN")
                        W = view(woff, [128, nk, 128], BF16)
                        ksplit = nk - 3 if m == 0 else nk
                        bks_ = {}
                        for blk in blocks:
                            c0, n, bi = blk["c0"], blk["n"], blk["idx"]
                            bk = banks.get()
                            bks_[bi] = bk
                            for k in range(ksplit):
                                tk = op(pe, "matmul", bk.ap[:, 0:n], lhsT=W[:, k, :], rhs=actT[:, k, c0:c0 + n], start=(k == 0),
                                        stop=(k == nk - 1), deps=[wtok, bk.free, act_tok[(k, bi)]] if k == 0 else [act_tok[(k, bi)]],
                                        mark=(k == nk - 1))
                            if ksplit == nk:
                                tu = x_update(bk.ap[:, 0:n], m, blk, L, "g2", [tk])
                                bk.free = TS(tu)
                                x_ready[bi] = TS(tu)
                                if half == 1:
                                    sacc.push(m, blk, tu)
                        if ksplit < nk:
                            for blk in blocks:
                                c0, n, bi = blk["c0"], blk["n"], blk["idx"]
                                bk = bks_[bi]
                                for k in range(ksplit, nk):
                                    tk = op(pe, "matmul", bk.ap[:, 0:n], lhsT=W[:, k, :], rhs=actT[:, k, c0:c0 + n], start=False,
                                            stop=(k == nk - 1), deps=[act_tok[(k, bi)]], mark=(k == nk - 1))
                                tu = x_update(bk.ap[:, 0:n], m, blk, L, "g2", [tk])
                                bk.free = TS(tu)
                                x_ready[bi] = TS(tu)
                                if half == 1:
                                    sacc.push(m, blk, tu)
                        ring.release(wi, tk)
            state["h_rd"] = TS(Tok("pe", None, pe.cnt))
            state["stage_free"] = TS(Tok("pe", None, pe.cnt), Tok("dve", None, dve.cnt), Tok("act", None, act.cnt))

        ffn(0)

        hw = rmsnorm(1, 1)
        P.tag = 'C'
        stage[0] = STAGE0
        gT = tnew([128, NCH, SBW], BF16)
        prP = [Res(tnew([128, 2 + 1024], F32)) for _ in range(2)]
        prS = [Res(tnew([128, NS, 10], F32)) for _ in range(2)]
        hxt = Rot([Res(tnew([128, 512], F32)) for _ in range(2)])
        cac = Rot([Res(tnew([128, 512], F32)) for _ in range(2)])
        sqr_c = Rot([Res(tnew([128, 512], BF16)) for _ in range(3)])
        sfree = TS(state["stage_free"])
        g_done = {}
        g_tok = {}
        histc = {}

        def init_hist_c(j_):
            if j_ >= 8 or j_ in histc:
                return
            pr_ = prP[j_ % 2]; prs_ = prS[j_ % 2]
            if sbi == 0:
                th_ = op(pool, "memset", pr_.ap[:, 0:2], 0.0, deps=[pr_.free, sfree])
            else:
                th_ = op(pool, "tensor_copy", out=pr_.ap[:, 0:2], in_=c_carry[:, j_, :], deps=[pr_.free, sfree])
                th_ = op(pool, "tensor_copy", out=prs_.ap[:, :, 0:2], in_=scT[:, j_, :].rearrange("p (q r) -> p q r", q=NS),
                         deps=[prs_.free, sfree, t_scT])
            histc[j_] = th_

        for j in range(8):
            wi, woff, wtok = ring.get("C")
            W = view(woff, [128, 8, 3, 128], BF16)
            pr = prP[j % 2]; prs = prS[j % 2]
            init_hist_c(j)
            init_hist_c(j + 1)
            th = histc[j]
            w0 = colv1[:, j, 39:40]; w1 = colv1[:, j, 40:41]; w2 = colv1[:, j, 41:42]
            for blk in blocks:
                c0, n, bi = blk["c0"], blk["n"], blk["idx"]
                bks = [banks.get() for _ in range(3)]
                for i in range(3):
                    for k in range(8):
                        tk = op(pe, "matmul", bks[i].ap[:, 0:n], lhsT=W[:, k, i, :], rhs=hT[:, k, c0:c0 + n], start=(k == 0),
                                stop=(k == 7), deps=[wtok, hw[bi], bks[i].free] if k == 0 else (), mark=(k == 7))
                hx = hxt.get(); ca = cac.get()
                if blk["kind"] == "p":
                    raw = pr.ap
                    cur_v = raw[:, 2 + c0:2 + c0 + n]; m1 = raw[:, 1 + c0:1 + c0 + n]; m2 = raw[:, c0:c0 + n]
                    f = lambda a: a
                else:
                    raw = prs.ap
                    cur_v = raw[:, :, 2:10]; m1 = raw[:, :, 1:9]; m2 = raw[:, :, 0:8]
                    f = q3
                e1 = op(act, "activation", out=hx.ap[:, 0:n], in_=bks[2].ap[:, 0:n], func=AF.Copy, deps=[tk, hx.free, sfree])
                bks[2].free = TS(e1)
                d1 = op(dve, "tensor_tensor", out=cur_v, in0=f(bks[1].ap[:, 0:n]), in1=f(hx.ap[:, 0:n]), op=ALU.mult, deps=[e1, th])
                bks[1].free = TS(d1); hx.free = TS(d1)
                d2 = op(dve, "tensor_scalar", out=f(ca.ap[:, 0:n]), in0=cur_v, scalar1=w2, scalar2=None, op0=ALU.mult,
                        deps=[d1, ca.free, t_colv1])
                d2 = op(dve, "scalar_tensor_tensor", out=f(ca.ap[:, 0:n]), in0=m1, scalar=w1, in1=f(ca.ap[:, 0:n]), op0=ALU.mult,
                        op1=ALU.add, deps=[d2])
                d2 = op(dve, "scalar_tensor_tensor", out=f(ca.ap[:, 0:n]), in0=m2, scalar=w0, in1=f(ca.ap[:, 0:n]), op0=ALU.mult,
                        op1=ALU.add, deps=[d2])
                d3 = op(dve, "tensor_tensor", out=gT[:, j, c0:c0 + n], in0=bks[0].ap[:, 0:n], in1=ca.ap[:, 0:n], op=ALU.mult,
                        deps=[d2, sfree])
                bks[0].free = TS(d3); ca.free = TS(d3)
                g_done[bi] = TS(d3)
                g_tok[(j, bi)] = d3
                if blk["kind"] == "p" and bi == 1:
                    pr.free = TS(op(pool, "tensor_copy", out=c_carry[:, j, :], in_=raw[:, 1024:1026], deps=[d2]))
                if blk["kind"] == "s":
                    prs.free = TS(op(pool, "tensor_copy", out=scT[:, j, :].rearrange("p (q r) -> p q r", q=NS), in_=raw[:, :, 8:10],
                                     deps=[d2]))
            ring.release(wi, tk)
        P.tag = 'WC'
        sacc = StatAcc(sqr_c)
        state["pre"] = sacc
        wc = [ring.get("WC") for _ in range(2)]
        Wc = [view(w_[1], [128, 8, 512], BF16) for w_ in wc]
        nT = early_T([tnew([128, 512], F32) for _ in range(6)])
        n_sfree = TS(sfree)
        n_hfree = TS(Tok("pe", None, pe.cnt))
        hw_e = {}
        pend_blk = None
        stp = None
        def wc_mm(blk, m, bk, klo, khi):
            c0, n, bi = blk["c0"], blk["n"], blk["idx"]
            p, mm = divmod(m, 4)
            tk_ = None
            for k in range(klo, khi):
                tk_ = op(pe, "matmul", bk.ap[:, 0:n], lhsT=Wc[p][:, k, mm * 128:(mm + 1) * 128], rhs=gT[:, k, c0:c0 + n],
                         start=(k == 0), stop=(k == 7),
                         deps=[wc[p][2], bk.free, g_tok[(k, bi)]] if k == 0 else [g_tok[(k, bi)]], mark=(k == 7))
            return tk_

        for blk in blocks:
            c0, n, bi = blk["c0"], blk["n"], blk["idx"]
            opened = {}
            if blk is blocks[0]:
                for m in range(3):
                    opened[m] = banks.get()
                    wc_mm(blk, m, opened[m], 0, 5)
            for m in range(8):
                if m in opened:
                    bk = opened[m]
                    tk = wc_mm(blk, m, bk, 5, 8)
                else:
                    bk = banks.get()
                    tk = wc_mm(blk, m, bk, 0, 8)
                tu = x_update(bk.ap[:, 0:n], m, blk, 1, "g1", [tk])
                bk.free = TS(tu)
                x_ready[bi] = TS(tu)
                sacc.push(m, blk, tu)
                if m == 0 and pend_blk is not None:
                    stp = Stepper(norm_gen(pend_blk, 1, 2, False, sacc, nT, n_sfree, n_hfree), hw_e, pend_blk["idx"])
                    pend_blk = None
                if stp is not None:
                    stp.step(2)
            if stp is not None:
                stp.finish()
                stp = None
            pend_blk = blk
        sacc.flush()
        hw_e[pend_blk["idx"]] = norm_block(pend_blk, 1, 2, False, sacc, nT, n_sfree, n_hfree)
        for w_ in wc:
            ring.release(w_[0], tk)
        state["hw_pre"] = hw_e
        state["h_rd"] = TS(Tok("pe", None, pe.cnt))
        state["stage_free"] = TS(Tok("pe", None, pe.cnt), Tok("dve", None, dve.cnt), Tok("act", None, act.cnt))

        ffn(1)

        rst = rmsnorm(0, 1, final=True)
        P.tag = 'final'
        yt = [Res(tnew([128, NCH, 128], F32)) for _ in range(3)]
        yst = [Res(tnew([128, D], F32)) for _ in range(4)]
        for ti in range(ntile):
            bi = min(ti // 4, 2)
            y, ty = rst[bi]
            yoff = (ti * 128) - blocks[bi]["c0"]
            t = yt[ti % 3]
            d1 = None
            for k_ in range(NCH):
                d1 = op(dve, "scalar_tensor_tensor", out=t.ap[:, k_, :], in0=xT[:, k_, ti * 128:(ti + 1) * 128],
                        scalar=colv1[:, k_, 4:5], in1=y.ap[:, yoff:yoff + 128], op0=ALU.mult, op1=ALU.mult,
                        deps=[ty, t.free] if k_ == 0 else ())
            ys = yst[ti % 4]
            ev = TS()
            for g in range(2):
                bk = banks.get()
                tk = None
                for jx in range(4):
                    k = g * 4 + jx
                    tk = op(pe, "transpose", bk.ap[:, jx * 128:(jx + 1) * 128], t.ap[:, k, :], ident_f,
                            deps=[d1, bk.free] if jx == 0 else (), mark=(jx == 3))
                te = op(act, "activation", out=ys.ap[:, g * 512:(g + 1) * 512], in_=bk.ap, func=AF.Copy, deps=[tk, ys.free])
                bk.free = TS(te)
                ev.add(te)
            t.free = TS(tk)
            dst = y_p[sbi * 1024 + ti * 128: sbi * 1024 + (ti + 1) * 128, :] if ti < 8 else y_s
            to = sp.dma(s_y[ti % 4], dst, ys.ap, deps=[ev])
            ys.free = TS(to)
            out_toks.add(to)
        state["stage_free"] = TS(Tok("pe", None, pe.cnt), Tok("dve", None, dve.cnt), Tok("act", None, act.cnt), out_toks)

    P.tag = 'stateout'
    stage = [XT0]

    def tnew2(shape, dt):
        esz = 2 if dt == BF16 else 4
        nb = int(np.prod(shape[1:])) * esz
        off = (stage[0] + 63) // 64 * 64
        stage[0] = off + nb
        assert stage[0] <= ARENA
        return view(off, shape, dt)

    fin = TS(state["stage_free"])
    o_nbp = tnew2([30, D], F32); o_nbs = tnew2([128, D], F32)
    o_ncp = tnew2([2, D], F32); o_ncs = tnew2([32, D], F32)
    o_nfp = tnew2([2, 2, 2 * DFF], F32)
    o_nfs = tnew2([32, 2, 2 * DFF], F32)

    def back(src_list, rows, dst_aps):
        for g in range(0, len(src_list), 4):
            grp = src_list[g:g + 4]
            bk = banks.get()
            tk = None
            for i, s in enumerate(grp):
                tk = op(pe, "transpose", bk.ap[0:rows, i * 128:(i + 1) * 128], s, ident_f,
                        deps=[fin, bk.free] if i == 0 else (), mark=(i == len(grp) - 1))
            te = op(act, "activation", out=dst_aps[g // 4], in_=bk.ap[0:rows, 0:len(grp) * 128], func=AF.Copy, deps=[tk, fin])
            bk.free = TS(te)
        return te

    te = back([tailb_p[:, j, 0:30] for j in range(8)], 30, [o_nbp[:, g * 512:(g + 1) * 512] for g in range(2)])
    out_toks.add(sp.dma(s_outmisc, nb_p, o_nbp, deps=[te]))
    te = back([tailb_s[:, j, :] for j in range(8)], 128, [o_nbs[:, g * 512:(g + 1) * 512] for g in range(2)])
    for qq in range(NS):
        out_toks.add(sp.dma(s_outmisc, nb_s[qq, 22:30, :], o_nbs[qq * 8:(qq + 1) * 8, :], deps=[te]))
    te = back([c_carry[:, j, :] for j in range(8)], 2, [o_ncp[:, g * 512:(g + 1) * 512] for g in range(2)])
    out_toks.add(sp.dma(s_outmisc, nc_p, o_ncp, deps=[te]))
    te = back([scT[:, j, :] for j in range(8)], 32, [o_ncs[:, g * 512:(g + 1) * 512] for g in range(2)])
    out_toks.add(sp.dma(s_outmisc, nc_s, o_ncs, deps=[te]))
    for L in range(2):
        te = back([f_carry[:, L, cj, :] for cj in range(44)], 2, [o_nfp[:, L, g * 512:min(2 * DFF, (g + 1) * 512)] for g in range(11)])
        out_toks.add(sp.dma(s_outmisc, nf_p[L], o_nfp[:, L, :], deps=[te]))
        te = back([sfT[:, L, cj, :] for cj in range(44)], 32, [o_nfs[:, L, g * 512:min(2 * DFF, (g + 1) * 512)] for g in range(11)])
        out_toks.add(sp.dma(s_outmisc, nf_s[L], o_nfs[:, L, :], deps=[te]))

    sp.add(lambda e: e.nop(), deps=[out_toks], mark=False)
    nc_ = P.finish()
    nc_._prog = P if False else None
    _NC_CACHE['prog'] = P
    return nc_


def _get_nc():
    if "nc" not in _NC_CACHE:
        _NC_CACHE["nc"] = build()
    return _NC_CACHE["nc"]


def kernel(x_prompt, x_sample, c_prompt, c_sample, state_conv_b, state_conv_c, state_ffn,
           w_in_ab, sgu_w, sgu_b, sgu_ln_g, sgu_ln_b, convb_w, convb_b, convb_ln_g, convb_ln_b,
           w_out_ab, w_in_c, convc_w, w_out_c, norm_mix_g, norm_ffn_g, ada_w, ada_b,
           ffn_up, ffn_conv_w, ffn_conv_b, ffn_down, final_g):
    f = lambda a: np.ascontiguousarray(np.asarray(a), dtype=np.float32)
    x_prompt, x_sample, c_prompt, c_sample = f(x_prompt), f(x_sample), f(c_prompt), f(c_sample)
    state_conv_b, state_conv_c, state_ffn = f(state_conv_b), f(state_conv_c), f(state_ffn)
    V1 = np.concatenate([f(norm_mix_g), f(norm_ffn_g), f(final_g)[None], f(convb_w)[0], f(convb_b), f(convb_ln_g),
                         f(convb_ln_b), f(convc_w)[0]], axis=0)
    assert V1.shape == (42, D)
    V2 = np.concatenate([f(ffn_conv_w).reshape(6, 2 * DFF), f(ffn_conv_b)], axis=0)
    V3 = f(ada_b)
    ln_gb = np.concatenate([f(sgu_ln_g), f(sgu_ln_b)], axis=0)
    ident = np.eye(128, dtype=np.float32)
    mask = np.triu(np.ones((128, 128), dtype=np.float32))
    shared = dict(w_in_ab=f(w_in_ab)[0], sgu_w=f(sgu_w)[0], sgu_b=f(sgu_b)[0], ln_gb=ln_gb, w_out_ab=f(w_out_ab)[0],
                  w_in_c=f(w_in_c)[0], w_out_c=f(w_out_c)[0], ada_w=f(ada_w), ffn_up=f(ffn_up), ffn_down=f(ffn_down),
                  V1=V1, V2=V2, V3=V3, ident=ident, mask=mask)
    in_maps = []
    for i in range(8):
        sl = slice(NS * i, NS * (i + 1))
        stb = np.zeros((NS, 32, D), np.float32)
        stb[:, :30] = state_conv_b[0, sl]
        m = dict(shared)
        m.update(xp=x_prompt[i], xs=x_sample[sl].reshape(128, D),
                 c17=np.concatenate([c_prompt[i:i + 1], c_sample[sl]], axis=0),
                 st_b=stb.reshape(512, D), st_b_raw=np.ascontiguousarray(state_conv_b[0, sl]),
                 st_c=state_conv_c[0, sl].reshape(32, D), st_f=state_ffn[:, sl].reshape(2, 32, 2 * DFF))
        in_maps.append(m)
    nc = _get_nc()
    res = run_bass_kernel_spmd(nc, in_maps, core_ids=list(range(8))).results
    g = lambda k: [np.asarray(r[k], dtype=np.float32) for r in res]
    y_prompt = np.stack(g("y_p"), 0)
    y_sample = np.concatenate([a.reshape(NS, TS_, D) for a in g("y_s")], 0)
    nb_p = np.stack(g("nb_p"), 0)[None]
    nc_p = np.stack(g("nc_p"), 0)[None]
    nf_p = np.stack(g("nf_p"), 1)
    nb_s = np.concatenate(g("nb_s"), 0)[None]
    nc_s = np.concatenate([a.reshape(NS, 2, D) for a in g("nc_s")], 0)[None]
    nf_s = np.concatenate([a.reshape(2, NS, 2, 2 * DFF) for a in g("nf_s")], 1)
    nv_s = np.concatenate([a.reshape(NS, TS_, D) for a in g("nv_s")], 0)[None]
    return (y_prompt, y_sample, nb_p, nc_p, nf_p, nb_s, nc_s, nf_s, nv_s)
```

```python
import contextlib
import math
import numpy as np
import concourse.bass as bass
import concourse.mybir as mybir
from concourse.bass_utils import run_bass_kernel_spmd

F32 = mybir.dt.float32
BF16 = mybir.dt.bfloat16
I32 = mybir.dt.int32
AF = mybir.ActivationFunctionType
ALU = mybir.AluOpType

D = 1024
DFF = 2816
NCH = 8
NFC = 22
EPS = 1e-6
SEQ = 2048
NS = 16
TS_ = 8
SBW = 1152
GC = 2.0 * math.sqrt(2.0 / math.pi)
SQ044 = math.sqrt(0.044715)


class Tok:
    __slots__ = ("key", "sem", "val")

    def __init__(self, key, sem, val):
        self.key = key
        self.sem = sem
        self.val = val


class TS:
    def __init__(self, *toks):
        self.d = {}
        self.add(*toks)

    def add(self, *toks):
        for t in toks:
            if t is None:
                continue
            if isinstance(t, TS):
                self.add(*t.d.values())
            elif isinstance(t, (list, tuple)):
                self.add(*t)
            else:
                c = self.d.get(t.key)
                if c is None or c.val < t.val:
                    self.d[t.key] = t
        return self

    def toks(self):
        return list(self.d.values())


class EngQ:
    def __init__(self, prog, name):
        self.prog = prog
        self.name = name
        self.ops = []
        self.cnt = 0
        self.sem = None
        self.seen = {}
        self.tags = []

    def _waits(self, deps, out):
        for t in deps:
            if t is None:
                continue
            if isinstance(t, TS):
                self._waits(t.toks(), out)
            elif isinstance(t, (list, tuple)):
                self._waits(t, out)
            else:
                if self.seen.get(t.key, 0) >= t.val:
                    continue
                self.seen[t.key] = t.val
                out.append(t)
        return out

    def add(self, fn, deps=(), mark=True):
        waits = self._waits(deps, [])
        tok = None
        if mark:
            self.cnt += 1
            tok = Tok(self.name, None, self.cnt)
        self.ops.append((waits, fn, mark, None))
        self.tags.append(self.prog.tag)
        return tok

    def dma(self, slot, out, in_, deps=(), **kw):
        waits = self._waits(deps, [])
        slot.cnt += 16
        tok = Tok(slot.key, slot, slot.cnt)
        self.ops.append((waits, lambda e: e.dma_start(out=out, in_=in_, **kw), False, slot))
        self.tags.append(self.prog.tag)
        return tok

    def emit(self, e):
        prog = self.prog
        for waits, fn, mark, slot in self.ops:
            for t in waits:
                sem = t.sem.sem if t.sem is not None else prog.q[t.key].sem
                e.wait_ge(sem, t.val)
            ins = fn(e)
            if slot is not None:
                ins.then_inc(slot.sem, 16)
            elif mark:
                ins.then_inc(self.sem, 1)


class DmaSlot:
    def __init__(self, key):
        self.key = key
        self.sem = None
        self.cnt = 0


class Prog:
    def __init__(self):
        self.nc = bass.Bass("TRN2", target_bir_lowering=False)
        self.q = {n: EngQ(self, n) for n in ("pe", "act", "dve", "pool", "sp")}
        self.slots = []
        self.stack = contextlib.ExitStack()
        self.tag = 'setup'

    def slot(self, name):
        s = DmaSlot("dma_" + name + "_" + str(len(self.slots)))
        self.slots.append(s)
        return s

    def sb(self, name, shape, dt):
        return self.stack.enter_context(self.nc.sbuf_tensor(name, list(shape), dt))

    def ps(self, name, shape, dt=F32):
        return self.stack.enter_context(self.nc.psum_tensor(name, list(shape), dt))

    def dram(self, name, shape, dt, kind):
        return self.nc.dram_tensor(name, list(shape), dt, kind=kind).ap()

    def finish(self):
        nc = self.nc
        for q in self.q.values():
            q.sem = self.stack.enter_context(nc.semaphore("s_" + q.name))
        for s in self.slots:
            s.sem = self.stack.enter_context(nc.semaphore(s.key))
        with nc.Block() as block:
            @block.tensor
            def _(e):
                self.q["pe"].emit(e)

            @block.scalar
            def _(e):
                self.q["act"].emit(e)

            @block.vector
            def _(e):
                self.q["dve"].emit(e)

            @block.gpsimd
            def _(e):
                self.q["pool"].emit(e)

            @block.sync
            def _(e):
                self.q["sp"].emit(e)
        self.stack.close()
        return nc


def op(q, method, *args, deps=(), mark=True, **kw):
    return q.add(lambda e: getattr(e, method)(*args, **kw), deps=deps, mark=mark)


class Res:
    def __init__(self, ap):
        self.ap = ap
        self.free = TS()


class Rot:
    def __init__(self, items):
        self.items = items
        self.all = list(items)
        self.i = 0

    def reserve(self, n):
        self.items = self.all[:len(self.all) - n]
        self.i = self.i % len(self.items)
        return self.all[len(self.all) - n:]

    def unreserve(self):
        self.items = list(self.all)

    def get(self):
        r = self.items[self.i]
        self.i = (self.i + 1) % len(self.items)
        return r


_NC_CACHE = {}


def build(debug=()):
    P = Prog()
    nc = P.nc
    pe, act, dve, pool, sp = (P.q[n] for n in ("pe", "act", "dve", "pool", "sp"))

    def din(name, shape):
        return P.dram(name, shape, F32, "ExternalInput")

    def dout(name, shape):
        return P.dram(name, shape, F32, "ExternalOutput")

    xp = din("xp", [SEQ, D]); xs = din("xs", [128, D]); c17 = din("c17", [17, D])
    st_b = din("st_b", [512, D])
    st_c = din("st_c", [32, D]); st_f = din("st_f", [2, 32, 2 * DFF])
    st_b_raw = din("st_b_raw", [NS, 30, D])
    w_in_ab = din("w_in_ab", [D, 4 * D]); sgu_w = din("sgu_w", [8, 128, 128]); sgu_b = din("sgu_b", [8, 128])
    ln_gb = din("ln_gb", [2, D])
    w_out_ab = din("w_out_ab", [2 * D, D]); w_in_c = din("w_in_c", [D, 3 * D]); w_out_c = din("w_out_c", [D, D])
    ada_w = din("ada_w", [2, D, 6 * D]); ffn_up = din("ffn_up", [2, D, 2 * DFF]); ffn_down = din("ffn_down", [2, DFF, D])
    V1 = din("V1", [42, D]); V2 = din("V2", [8, 2 * DFF]); V3 = din("V3", [2, 6 * D])
    ident_d = din("ident", [128, 128]); mask_d = din("mask", [128, 128])

    y_p = dout("y_p", [SEQ, D]); y_s = dout("y_s", [128, D])
    nb_p = dout("nb_p", [30, D]); nc_p = dout("nc_p", [2, D]); nf_p = dout("nf_p", [2, 2, 2 * DFF])
    nb_s = dout("nb_s", [NS, 30, D]); nc_s = dout("nc_s", [32, D]); nf_s = dout("nf_s", [2, 32, 2 * DFF])
    nv_s = dout("nv_s", [128, D])

    ARENA = 212800
    arena = P.sb("arena", [128, ARENA // 4], F32)[:]
    cur = [0]

    def alloc(nbytes):
        off = (cur[0] + 63) // 64 * 64
        cur[0] = off + nbytes
        assert cur[0] <= ARENA, ("SBUF overflow", cur[0], ARENA)
        return off

    def view(off, shape, dt):
        esz = 2 if dt == BF16 else 4
        nel = int(np.prod(shape[1:]))
        nw = (nel * esz + 3) // 4
        ap = arena[0:shape[0], off // 4: off // 4 + nw]
        if dt != F32:
            ap = ap.bitcast(dt)
            if esz == 2:
                ap = ap[:, 0:nel]
        if len(shape) == 3:
            ap = ap.rearrange("p (a b) -> p a b", a=shape[1])
        elif len(shape) == 4:
            ap = ap.rearrange("p (a b c) -> p a b c", a=shape[1], b=shape[2])
        return ap

    def new(shape, dt):
        esz = 2 if dt == BF16 else 4
        return view(alloc(int(np.prod(shape[1:])) * esz), shape, dt)

    ident_f = new([128, 128], F32); mask = new([128, 128], F32)
    ones_bf = new([128, 128], BF16); ones128 = new([128, 128], BF16)
    colv1 = new([128, NCH, 42], F32); colv2 = new([128, 44, 8], F32)
    modT = new([128, 2, 48, 17], F32)
    WT_p = new([128, 8, 128], BF16); WT_s = new([128, 8, 128], BF16)
    sel16 = new([16, 8, 128], BF16); bias16 = new([16, 2, 128], BF16)
    lnbc = new([128, 2, D], F32)
    glu_carry = new([128, NCH, 30], BF16); c_carry = new([128, NCH, 2], F32)
    f_carry = new([128, 2, 44, 2], F32)
    sfT = new([128, 2, 44, 32], F32); scT = new([128, NCH, 32], F32)
    tailb_p = new([128, NCH, 32], F32); tailb_s = new([128, NCH, 128], F32)
    epsc = new([128, 1], F32)
    xu_tmp = new([128, 128], F32)
    adabT = new([128, 48, 2], F32)
    cT = new([128, 8, 128], BF16)
    NSLOT = 4
    SLOTB = 8192
    ring_off = [alloc(SLOTB) for _ in range(NSLOT)]
    XT0 = (cur[0] + 63) // 64 * 64
    xT = new([128, NCH, SBW], F32)
    hT = new([128, NCH, SBW], BF16)
    STAGE0 = cur[0]

    banks = Rot([Res(P.ps("ps%d" % i, [128, 512])[:]) for i in range(8)])

    def slot(name):
        return P.slot(name)

    s_const = slot("const")
    s_outmisc = slot("outmisc")
    out_toks = TS()

    class Ring:
        def __init__(self):
            self.slots = [slot("ring%d" % i) for i in range(NSLOT)]
            self.free = [TS() for _ in range(NSLOT)]
            self.sched = []
            self.tok = {}
            self.nxt = 0
            self.issued = 0

        def plan(self, name, dmas):
            self.sched.append((name, dmas))

        def _issue(self, i):
            name, dmas = self.sched[i]
            s = i % NSLOT
            tok = None
            for dst, src in dmas(ring_off[s]):
                tok = pool.dma(self.slots[s], dst, src, deps=[self.free[s]])
            self.tok[i] = tok

        def start(self):
            self.busy = [False] * NSLOT
            self._pump()

        def _pump(self):
            while self.issued < len(self.sched) and not self.busy[self.issued % NSLOT]:
                self.busy[self.issued % NSLOT] = True
                self._issue(self.issued)
                self.issued += 1

        def get(self, name):
            i = self.nxt
            self.nxt += 1
            assert self.sched[i][0] == name, (self.sched[i][0], name)
            assert i < self.issued, "ring: piece not issued (held too many)"
            return i, ring_off[i % NSLOT], self.tok[i]

        def release(self, i, tok):
            self.free[i % NSLOT] = TS(tok)
            self.busy[i % NSLOT] = False
            self._pump()

    ring = Ring()

    def wsrc(w2d, c0, n):
        return w2d.rearrange("(k p) n -> p k n", p=128)[:, :, c0:c0 + n]

    def plan_all():
        for nb in range(12):
            ring.plan("ada", lambda off, nb=nb: [(view(off, [128, 8, 512], BF16), wsrc(ada_w[0], nb * 512, 512))])
        ada1 = [0]

        def plan_ada1():
            nb = ada1[0]
            if nb < 12:
                ring.plan("ada", lambda off, nb=nb: [(view(off, [128, 8, 512], BF16), wsrc(ada_w[1], nb * 512, 512))])
                ada1[0] += 1
        for sbi in range(2):
            ring.plan("U", lambda off: [(view(off, [128, 8, 512], BF16), wsrc(w_in_ab, 0, 512))])
            for hh in range(2):
                ring.plan("V", lambda off, hh=hh: [(view(off, [128, 8, 512], BF16), wsrc(w_in_ab, D + hh * 512, 512))])
            ring.plan("U", lambda off: [(view(off, [128, 8, 512], BF16), wsrc(w_in_ab, 512, 512))])
            for p in range(4):
                ring.plan("AG", lambda off, p=p: [
                    (view(off, [128, 8, 2, 256], BF16)[:, :, 0, :], wsrc(w_in_ab, 2 * D + p * 256, 256)),
                    (view(off, [128, 8, 2, 256], BF16)[:, :, 1, :], wsrc(w_in_ab, 3 * D + p * 256, 256))])
            if sbi == 0:
                for _ in range(12):
                    plan_ada1()
            for p in range(4):
                ring.plan("WO", lambda off, p=p: [(view(off, [128, 16, 256], BF16), wsrc(w_out_ab, p * 256, 256))])
            for L in range(2):
                if L == 1:
                    for j in range(8):
                        ring.plan("C", lambda off, j=j: [
                            (view(off, [128, 8, 3, 128], BF16)[:, :, i, :], wsrc(w_in_c, i * D + j * 128, 128)) for i in range(3)])
                    for p in range(2):
                        ring.plan("WC", lambda off, p=p: [(view(off, [128, 8, 512], BF16), wsrc(w_out_c, p * 512, 512))])
                for half in range(2):
                    k0, nk = (0, 12) if half == 0 else (12, 10)
                    for p in range(nk // 2):
                        ring.plan("UP", lambda off, L=L, c=k0 + 2 * p: [
                            (view(off, [128, 8, 2, 256], BF16)[:, :, 0, :], wsrc(ffn_up[L], c * 128, 256)),
                            (view(off, [128, 8, 2, 256], BF16)[:, :, 1, :], wsrc(ffn_up[L], DFF + c * 128, 256))])
                    if half == 1:
                        for (m0, cnt) in ((0, 3), (3, 3), (6, 2)):
                            ring.plan("DN", lambda off, L=L, m0=m0, cnt=cnt, k0=k0, nk=nk: [
                                (view(off, [128, nk, cnt * 128], BF16),
                                 ffn_down[L].rearrange("(k p) n -> p k n", p=128)[:, k0:k0 + nk, m0 * 128:(m0 + cnt) * 128])])
                    else:
                        for m in range(8):
                            ring.plan("DN", lambda off, L=L, m=m, k0=k0, nk=nk: [
                                (view(off, [128, nk, 128], BF16),
                                 ffn_down[L].rearrange("(k p) n -> p k n", p=128)[:, k0:k0 + nk, m * 128:(m + 1) * 128])])

    plan_all()
    ring.start()

    def newton(a, y, t, iters, deps, q=None):
        q = q or dve
        tk = op(dve, "tensor_scalar", out=y.bitcast(I32), in0=a.bitcast(I32), scalar1=-0.5, scalar2=1597463007.0,
                op0=ALU.mult, op1=ALU.add, deps=deps)
        for _ in range(iters):
            tk = op(dve, "tensor_tensor", out=t, in0=a, in1=y, op=ALU.mult, deps=[tk])
            tk = op(dve, "scalar_tensor_tensor", out=t, in0=t, scalar=-0.5, in1=y, op0=ALU.mult, op1=ALU.mult, deps=[tk])
            tk = op(dve, "scalar_tensor_tensor", out=y, in0=t, scalar=1.5, in1=y, op0=ALU.add, op1=ALU.mult, deps=[tk])
        return tk

    def q3(ap2d):
        return ap2d.rearrange("p (q t) -> p q t", q=NS)

    def view_of(t3, j0, shape):
        return t3[0:shape[0], j0:j0 + 4, :].rearrange("p a b -> p (a b)")

    def bc_s(col17):
        return col17[:, 1:17].unsqueeze(2).to_broadcast([128, NS, TS_])

    t_id = sp.dma(s_const, ident_f, ident_d)
    t_mk = sp.dma(s_const, mask, mask_d)
    t_ln = sp.dma(s_const, lnbc[:, 0, :], ln_gb[0:1, :].to_broadcast([128, D]))
    t_ln = sp.dma(s_const, lnbc[:, 1, :], ln_gb[1:2, :].to_broadcast([128, D]))
    t_const = TS(t_ln)
    t_ones = op(dve, "memset", ones_bf, 1.0)
    t_ones = op(dve, "memset", ones128, 1.0 / 128)
    t_eps = op(dve, "memset", epsc, EPS)
    scr = [STAGE0]

    def snew(shape, dt):
        esz = 2 if dt == BF16 else 4
        nb = int(np.prod(shape[1:])) * esz
        off = (scr[0] + 63) // 64 * 64
        scr[0] = off + nb
        assert scr[0] <= ARENA, "setup scratch overflow"
        return view(off, shape, dt)

    xT_off_scr = [None]

    s_set = slot("setup")
    bias8 = snew([8, 2, 128], F32); bhi = snew([8, 2, 128], BF16); blo = snew([8, 2, 128], BF16); bhf = snew([8, 2, 128], F32)
    s_b8 = slot("b8")
    t_b8 = sp.dma(s_b8, bias8[:, 0, :], sgu_b)
    with nc.allow_non_contiguous_dma(reason="tiny bias broadcast"):
        t_b8 = sp.dma(s_b8, bias8[:, 1, :].rearrange("p (q t) -> p q t", q=NS),
                      sgu_b[:, 0:8].unsqueeze(1).to_broadcast([8, NS, 8]))
    t_sel = op(dve, "tensor_copy", out=sel16[0:8], in_=ident_f[0:8, 0:8].unsqueeze(2).to_broadcast([8, 8, 128]), deps=[t_const])
    t1 = op(dve, "tensor_copy", out=bhi, in_=bias8, deps=[t_b8])
    t1 = op(dve, "tensor_copy", out=bhf, in_=bhi, deps=[t1])
    t1 = op(dve, "tensor_tensor", out=blo, in0=bias8, in1=bhf, op=ALU.subtract, deps=[t1])
    s_sel = slot("sel")
    t2 = act.dma(s_sel, bias16[0:8], bhi, deps=[t1])
    t2 = act.dma(s_sel, bias16[8:16], blo, deps=[t1])
    t2 = act.dma(s_sel, sel16[8:16], sel16[0:8], deps=[t_sel])
    t_sel = TS(t2)

    v1s = snew([42, D], F32); big22 = snew([32, 2 * DFF], F32); v2s = big22[0:8, :]
    c17s = snew([17, D], F32); csig = snew([17, D], F32)
    mtok = [Res(view_of(tailb_s, 0, [17, 512])), Res(view_of(tailb_s, 4, [17, 512]))]
    sguw = snew([128, 8, 128], F32); sguw2 = snew([128, 8, 128], F32)
    v3s = xT[0:2, :, :].rearrange("p a b -> p (a b)")[:, 0:6 * D]
    stcs = snew([32, D], F32)
    s_big = slot("big")
    s_set2 = slot("setup2")
    t_v1 = sp.dma(s_set, v1s, V1)
    t_c = sp.dma(s_set, c17s, c17)
    t_sw = sp.dma(s_set, sguw, sgu_w.rearrange("h t s -> t h s"))
    t_v2 = sp.dma(s_big, v2s, V2)
    t_v3 = sp.dma(s_set2, v3s, V3)
    t_set = TS(t_sw)
    t_set2 = TS(t_v3)
    t_z = op(dve, "memset", sguw2, 0.0)
    s_set3 = slot("setup3")
    t_bd = None
    with nc.allow_non_contiguous_dma(reason="8x8 blocks"):
        for qq in range(NS):
            t_bd = act.dma(s_set3, sguw2[qq * 8:(qq + 1) * 8, :, qq * 8:(qq + 1) * 8],
                          sgu_w[:, 0:8, 0:8].rearrange("h t s -> t h s"), deps=[t_z])

    def transposes_to(bank, srcs, width, deps):
        tk = None
        for i, s in enumerate(srcs):
            rows = s.shape[0]
            tk = op(pe, "transpose", bank.ap[:, i * width:i * width + rows], s, ident_f[0:rows, 0:rows],
                    deps=[deps, bank.free, t_const] if i == 0 else (), mark=(i == len(srcs) - 1))
        return tk

    bk = banks.get()
    tk = transposes_to(bk, [v1s[:, k * 128:(k + 1) * 128] for k in range(8)], 42, [t_set])
    t_colv1 = op(dve, "tensor_copy", out=colv1, in_=bk.ap[:, 0:336].rearrange("p (a b) -> p a b", a=8), deps=[tk])
    bk.free = TS(t_colv1)
    bk = banks.get()
    tk = transposes_to(bk, [v2s[:, k * 128:(k + 1) * 128] for k in range(44)], 8, [t_v2])
    big_free = TS(tk)
    t_colv2 = op(dve, "tensor_copy", out=colv2, in_=bk.ap[:, 0:352].rearrange("p (a b) -> p a b", a=44), deps=[tk])
    bk.free = TS(t_colv2)
    bk = banks.get()
    tk = transposes_to(bk, [v3s[:, k * 128:(k + 1) * 128] for k in range(48)], 2, [t_set2])
    t_adab = op(dve, "tensor_copy", out=adabT, in_=bk.ap[:, 0:96].rearrange("p (a b) -> p a b", a=48), deps=[tk])
    bk.free = TS(t_adab)
    t1 = op(act, "activation", out=csig, in_=c17s, func=AF.Sigmoid, deps=[t_set])
    t2 = op(dve, "tensor_tensor", out=csig, in0=csig, in1=c17s, op=ALU.mult, deps=[t1])
    bk = banks.get()
    tk = transposes_to(bk, [csig[:, k * 128:(k + 1) * 128] for k in range(8)], 17, [t2])
    t_cz = op(dve, "memset", cT, 0.0)
    t_cT = op(dve, "tensor_copy", out=cT[:, :, 0:17], in_=bk.ap[:, 0:136].rearrange("p (a b) -> p a b", a=8), deps=[tk, t_cz])
    bk.free = TS(t_cT)
    for (src, dst, dep) in ((sguw, WT_p, t_set), (sguw2, WT_s, TS(t_bd))):
        for g in range(2):
            bk = banks.get()
            tk = transposes_to(bk, [src[:, g * 4 + i, :] for i in range(4)], 128, [dep])
            tw = op(dve, "tensor_tensor", out=dst[:, g * 4:(g + 1) * 4, :], in0=bk.ap.rearrange("p (a b) -> p a b", a=4),
                    in1=mask.unsqueeze(1).to_broadcast([128, 4, 128]), op=ALU.mult, deps=[tk])
            bk.free = TS(tw)
    t_WT = tw
    t_idc = op(dve, "tensor_scalar", out=mask, in0=ident_f, scalar1=-1.0 / 128, scalar2=None, op0=ALU.add, deps=[t_WT, t_const])
    identc = mask
    cb_bf = snew([128, NCH], BF16)
    t1 = op(dve, "tensor_copy", out=cb_bf, in_=colv1[:, :, 36], deps=[t_colv1])
    bk = banks.get()
    tk = op(pe, "matmul", bk.ap[:, 0:NCH], lhsT=ones128, rhs=cb_bf, start=True, stop=True, deps=[t1, bk.free, t_ones])
    t_cbc = op(dve, "tensor_tensor", out=colv1[:, :, 36], in0=colv1[:, :, 36], in1=bk.ap[:, 0:NCH], op=ALU.subtract, deps=[tk])
    bk.free = TS(t_cbc)
    LATE0 = (ARENA - (32 * 1024)) // 64 * 64
    late_f = view(LATE0, [32, 2 * DFF], F32)
    late_c = view(LATE0 + 2 * DFF * 4 + 64, [32, D], F32)
    s_late = slot("late")
    s_late2 = slot("late2")

    def do_states(dep):
        t_c_ = sp.dma(s_late2, late_c, st_c, deps=[dep])
        free_ = TS(dep)
        t_sf = None
        for L in range(2):
            t_ld = sp.dma(s_late, late_f, st_f[L], deps=[free_])
            for g in range(3):
                ks = list(range(g * 16, min(44, g * 16 + 16)))
                bk = banks.get()
                tk = transposes_to(bk, [late_f[:, k * 128:(k + 1) * 128] for k in ks], 32, [t_ld])
                t_sf = op(act, "activation", out=sfT[:, L, ks[0]:ks[-1] + 1, :],
                          in_=bk.ap[:, 0:len(ks) * 32].rearrange("p (a b) -> p a b", a=len(ks)), func=AF.Copy, deps=[tk])
                bk.free = TS(t_sf)
            free_ = TS(tk)
        bk = banks.get()
        tk = transposes_to(bk, [late_c[:, k * 128:(k + 1) * 128] for k in range(8)], 32, [t_c_])
        t_sc = op(act, "activation", out=scT, in_=bk.ap[:, 0:256].rearrange("p (a b) -> p a b", a=8), func=AF.Copy, deps=[tk])
        bk.free = TS(t_sc)
        return t_sf, t_sc

    t_sfT = None
    t_scT = None

    def ada_step(L, nb, mt, defer=False):
        wi, woff, wtok = ring.get("ada")
        W = view(woff, [128, 8, 512], BF16)
        bk = banks.get()
        for k in range(8):
            tk = op(pe, "matmul", bk.ap[:, :], lhsT=cT[:, k, :], rhs=W[:, k, :], start=(k == 0), stop=(k == 7),
                    deps=[wtok, t_cT, bk.free] if k == 0 else (), mark=(k == 7))
        ring.release(wi, tk)
        te = op(act, "activation", out=mt.ap, in_=bk.ap[0:17, :], func=AF.Copy, deps=[tk, mt.free])
        bk.free = TS(te)

        def part2():
            bk2 = banks.get()
            tk2 = transposes_to(bk2, [mt.ap[:, i * 128:(i + 1) * 128] for i in range(4)], 17, [te])
            mt.free = TS(tk2)
            tm = op(dve, "tensor_tensor", out=modT[:, L, nb * 4:(nb + 1) * 4, :],
                    in0=bk2.ap[:, 0:68].rearrange("p (a b) -> p a b", a=4),
                    in1=adabT[:, nb * 4:(nb + 1) * 4, L:L + 1].to_broadcast([128, 4, 17]), op=ALU.add, deps=[tk2, t_adab])
            bk2.free = TS(tm)
            return tm
        if defer:
            return part2
        return part2()

    def ada_finish(L, t_mod):
        for (lo, vi) in ((8, L), (32, 2 + L)):
            t_mod = op(dve, "scalar_tensor_tensor", out=modT[:, L, lo:lo + 8, :], in0=modT[:, L, lo:lo + 8, :], scalar=1.0,
                       in1=colv1[:, :, vi:vi + 1].to_broadcast([128, 8, 17]), op0=ALU.add, op1=ALU.mult, deps=[t_mod, t_colv1])
        return t_mod

    def do_ada0():
        t_mod = None
        for nb in range(12):
            t_mod = ada_step(0, nb, mtok[nb % 2])
        return ada_finish(0, t_mod)
    setup_done = TS(t_idc, t_cbc, t_adab, t_cT, t_WT, t_colv2, t_sel, t_ones, t_eps, Tok("pe", None, pe.cnt), Tok("act", None, act.cnt))

    MOD = dict(sh1=0, gm1=8, g1=16, sh2=24, gm2=32, g2=40)

    def modk(L, name, k):
        return modT[:, L, MOD[name] + k, :]

    s_x = [slot("x%d" % i) for i in range(4)]
    s_y = [slot("y%d" % i) for i in range(4)]
    s_nv = slot("nv")
    s_stb = slot("stb")
    s_stb_b = slot("stb_b")

    state = dict(x_ready={}, h_rd=TS(setup_done), stage_free=TS(setup_done))

    def blocks_of(sbi):
        bl = [dict(kind="p", c0=0, n=512, idx=0), dict(kind="p", c0=512, n=512, idx=1)]
        if sbi == 1:
            bl.append(dict(kind="s", c0=1024, n=128, idx=2))
        return bl

    def bv(ap2d, blk):
        return ap2d if blk["kind"] == "p" else q3(ap2d)

    for sbi in range(2):
        blocks = blocks_of(sbi)
        ntile = 8 + (1 if sbi == 1 else 0)
        x_ready = {}
        stage = [STAGE0]

        def tnew(shape, dt):
            esz = 2 if dt == BF16 else 4
            nb = int(np.prod(shape[1:])) * esz
            off = (stage[0] + 63) // 64 * 64
            stage[0] = off + nb
            assert stage[0] <= ARENA, ("stage overflow", stage[0], ARENA)
            return view(off, shape, dt)

        def norm_gen(blk, L, which, final, pre, T, sfree, h_free):
            c0, n, bi = blk["c0"], blk["n"], blk["idx"]
            sq, av, yv, tv, tks = T["sq"], T["av"], T["yv"], T["tv"], T["tks"]
            if pre is not None:
                bk = pre.banks[bi]; tk = pre.tok[bi]
            else:
                tsq = op(act, "activation", out=sq.ap[:, :, 0:n], in_=xT[:, :, c0:c0 + n], func=AF.Square,
                         deps=[x_ready[bi], sq.free, sfree])
                bk = banks.get()
                for k in range(8):
                    tk = op(pe, "matmul", bk.ap[:, 0:n], lhsT=ones_bf, rhs=sq.ap[:, k, 0:n], start=(k == 0), stop=(k == 7),
                            deps=[tsq, bk.free] if k == 0 else (), mark=(k == 7))
                sq.free = TS(tk)
            a = av[bi % len(av)]; y = yv[bi % len(yv)]
            aa = a.ap[:, 0:n]; yy = y.ap[:, 0:n]; tt_ = tv[:, 0:n]
            ta = op(act, "activation", out=aa, in_=bk.ap[:, 0:n], func=AF.Identity, bias=epsc[:, 0:1], scale=1.0 / D,
                    deps=[tk, a.free, sfree, t_eps])
            bk.free = TS(ta)
            ty = op(dve, "tensor_scalar", out=yy.bitcast(I32), in0=aa.bitcast(I32), scalar1=-0.5, scalar2=1597463007.0,
                    op0=ALU.mult, op1=ALU.add, deps=[ta, y.free])
            yield
            for _ in range(2):
                ty = op(dve, "tensor_tensor", out=tt_, in0=aa, in1=yy, op=ALU.mult, deps=[ty])
                ty = op(dve, "scalar_tensor_tensor", out=tt_, in0=tt_, scalar=-0.5, in1=yy, op0=ALU.mult, op1=ALU.mult, deps=[ty])
                ty = op(dve, "scalar_tensor_tensor", out=yy, in0=tt_, scalar=1.5, in1=yy, op0=ALU.add, op1=ALU.mult, deps=[ty])
                yield
            a.free = TS(ty)
            if final:
                return (y, ty)
            hdone = TS()
            if blk["kind"] == "s":
                gb = MOD["gm%d" % which]; sb_ = MOD["sh%d" % which]
                for hf in range(2):
                    t = tks[hf]
                    t3 = t.ap[:, 0:512].rearrange("p (a b) -> p a b", a=4)
                    t4 = t.ap[:, 0:512].rearrange("p (a q t) -> p a q t", a=4, q=NS)
                    xv = xT[:, 4 * hf:4 * hf + 4, c0:c0 + n]
                    t1 = op(dve, "tensor_tensor", out=t3, in0=xv, in1=y.ap[:, 0:n].unsqueeze(1).to_broadcast([128, 4, n]),
                            op=ALU.mult, deps=[ty, t.free])
                    t1 = op(dve, "tensor_tensor", out=t4, in0=t4,
                            in1=modT[:, L, gb + 4 * hf:gb + 4 * hf + 4, 1:17].unsqueeze(3).to_broadcast([128, 4, NS, TS_]),
                            op=ALU.mult, deps=[t1])
                    t2 = op(dve, "tensor_tensor", out=hT[:, 4 * hf:4 * hf + 4, c0:c0 + n].rearrange("p a (q t) -> p a q t", q=NS),
                            in0=t4, in1=modT[:, L, sb_ + 4 * hf:sb_ + 4 * hf + 4, 1:17].unsqueeze(3).to_broadcast([128, 4, NS, TS_]),
                            op=ALU.add, deps=[t1, h_free])
                    t.free = TS(t2)
                    hdone.add(t2)
                    if hf == 0:
                        yield
                y.free = TS(hdone)
                return hdone
            for k in range(8):
                xv = xT[:, k, c0:c0 + n]
                t = tks[k % 3]
                if blk["kind"] == "p":
                    t1 = op(dve, "scalar_tensor_tensor", out=t.ap[:, 0:n], in0=xv, scalar=modk(L, "gm%d" % which, k)[:, 0:1],
                            in1=y.ap[:, 0:n], op0=ALU.mult, op1=ALU.mult, deps=[ty, t.free])
                    t2 = op(act, "activation", out=hT[:, k, c0:c0 + n], in_=t.ap[:, 0:n], func=AF.Identity,
                            bias=modk(L, "sh%d" % which, k)[:, 0:1], scale=1.0, deps=[t1, h_free])
                    t.free = TS(t2)
                    hdone.add(t2)
                else:
                    t1 = op(dve, "tensor_tensor", out=t.ap[:, 0:n], in0=xv, in1=y.ap[:, 0:n], op=ALU.mult, deps=[ty, t.free])
                    t1 = op(dve, "tensor_tensor", out=q3(t.ap[:, 0:n]), in0=q3(t.ap[:, 0:n]),
                            in1=bc_s(modk(L, "gm%d" % which, k)), op=ALU.mult, deps=[t1])
                    t2 = op(dve, "tensor_tensor", out=q3(hT[:, k, c0:c0 + n]), in0=q3(t.ap[:, 0:n]),
                            in1=bc_s(modk(L, "sh%d" % which, k)), op=ALU.add, deps=[t1, h_free])
                    t.free = TS(t2)
                    hdone.add(t2)
                if k < 7:
                    yield
            y.free = TS(hdone)
            return hdone

        def norm_block(*args):
            g = norm_gen(*args)
            while True:
                try:
                    next(g)
                except StopIteration as e_:
                    return e_.value

        class Stepper:
            def __init__(self, g, dst, key):
                self.g, self.dst, self.key, self.done = g, dst, key, False

            def step(self, nunits):
                for _ in range(nunits):
                    if self.done:
                        return
                    try:
                        next(self.g)
                    except StopIteration as e_:
                        self.dst[self.key] = e_.value
                        self.done = True

            def finish(self):
                while not self.done:
                    self.step(1)

        P.tag = 'A_loadx'
        xst = [Res(tnew([128, D], F32)) for _ in range(4)]
        prev_free = TS(state["stage_free"])
        T0 = dict(sq=Res(tnew([128, NCH, 512], BF16)), av=[Res(tnew([128, 512], F32)) for _ in range(2)],
                  yv=[Res(tnew([128, 512], F32)) for _ in range(2)], tv=tnew([128, 512], F32),
                  tks=[Res(tnew([128, 512], F32)) for _ in range(3)])
        hw_first = {}
        first_steppers = []
        units_left = {}
        xw = TS()
        ada0_nb = [0]
        ada0_tok = [None]

        def ada0_some(k_):
            for _ in range(k_):
                if ada0_nb[0] < 12:
                    ada0_tok[0] = ada_step(0, ada0_nb[0], mtok[ada0_nb[0] % 2])
                    ada0_nb[0] += 1

        for ti in range(ntile):
            if sbi == 0:
                ada0_some(1)
            src = xp[sbi * 1024 + ti * 128: sbi * 1024 + (ti + 1) * 128, :] if ti < 8 else xs
            st = xst[ti % 4]
            tl = sp.dma(s_x[ti % 4], st.ap, src, deps=[st.free, prev_free])
            for g in range(2):
                bk = banks.get()
                tk = None
                for j in range(4):
                    k = g * 4 + j
                    tk = op(pe, "transpose", bk.ap[:, j * 128:(j + 1) * 128], st.ap[:, k * 128:(k + 1) * 128], ident_f,
                            deps=[tl, bk.free, setup_done] if j == 0 else (), mark=(j == 3))
                dst = xT[:, g * 4:(g + 1) * 4, ti * 128:(ti + 1) * 128]
                srcv = bk.ap.rearrange("p (a b) -> p a b", a=4)
                if g == 0:
                    te = op(act, "activation", out=dst, in_=srcv, func=AF.Copy, deps=[tk, prev_free])
                else:
                    te = op(dve, "tensor_copy", out=dst, in_=srcv, deps=[tk, prev_free])
                bk.free = TS(te)
                xw.add(te)
                if g == 1:
                    st.free = TS(tk)
            for stp_ in first_steppers:
                k_ = min(3, units_left[id(stp_)])
                stp_.step(k_)
                units_left[id(stp_)] -= k_
            blk_done = next((b for b in blocks if b["c0"] + b["n"] == (ti + 1) * 128), None)
            if blk_done is not None:
                x_ready[blk_done["idx"]] = TS(xw)
                stp_ = Stepper(norm_gen(blk_done, 0, 1, False, None, T0, prev_free, state["h_rd"]), hw_first, blk_done["idx"])
                first_steppers.append(stp_)
                units_left[id(stp_)] = 3 if sbi == 0 else 99
        if sbi == 0:
            for stp_ in first_steppers:
                stp_.step(units_left[id(stp_)])
            P.tag = 'ada0'
            ada0_some(12)
            t_mod0 = ada_finish(0, ada0_tok[0])
            t_sfT, t_scT = do_states(TS(setup_done, prev_free))
        P.tag = 'rmsnorm'
        for stp_ in first_steppers:
            stp_.finish()
        state["hw_pre"] = hw_first
        state["stage_free"] = TS(Tok("pe", None, pe.cnt), Tok("act", None, act.cnt), Tok("dve", None, dve.cnt))

        def rmsnorm(L, which, final=False):
            hw_pre = state.pop("hw_pre", None)
            if hw_pre is not None:
                state.pop("pre", None)
                banks.unreserve()
                P.tag = 'rmsnorm'
                stage[0] = STAGE0
                state["stage_free"] = TS(Tok("pe", None, pe.cnt), Tok("dve", None, dve.cnt), Tok("act", None, act.cnt))
                return hw_pre
            pre = state.pop("pre", None)
            if pre is not None:
                pre.flush()
            P.tag = 'rmsnorm'
            stage[0] = STAGE0
            T = dict(sq=Res(tnew([128, NCH, 512], BF16)), av=[Res(tnew([128, 512], F32)) for _ in range(2)],
                     yv=[Res(tnew([128, 512], F32)) for _ in range(3 if final else 2)], tv=tnew([128, 512], F32),
                     tks=[Res(tnew([128, 512], F32)) for _ in range(3)])
            sfree = TS(state["stage_free"])
            hw = {}
            for blk in blocks:
                hw[blk["idx"]] = norm_block(blk, L, which, final, pre, T, sfree, state["h_rd"])
            if pre is not None:
                banks.unreserve()
            state["stage_free"] = TS(Tok("pe", None, pe.cnt), Tok("dve", None, dve.cnt), Tok("act", None, act.cnt))
            return hw

        def early_T(tiles):
            return dict(sq=None, av=[Res(tiles[0])], yv=[Res(tiles[1])], tv=tiles[2], tks=[Res(tiles[3]), Res(tiles[4]), Res(tiles[5])])

        def gelu(ps_ap, out_ap, n, deps, out_free=None):
            return op(act, "activation", out=out_ap, in_=ps_ap, func=AF.Gelu_apprx_tanh, deps=[deps, out_free])

        def x_update(ps_ap, m, blk, L, gname, deps):
            c0, n = blk["c0"], blk["n"]
            xv = xT[:, m, c0:c0 + n]
            g = modk(L, gname, m)
            if blk["kind"] == "p":
                return op(dve, "scalar_tensor_tensor", out=xv, in0=ps_ap, scalar=g[:, 0:1], in1=xv, op0=ALU.mult, op1=ALU.add,
                          deps=deps)
            t1 = op(dve, "tensor_tensor", out=q3(xu_tmp[:, 0:n]), in0=q3(ps_ap), in1=bc_s(g), op=ALU.mult, deps=deps)
            return op(dve, "tensor_tensor", out=xv, in0=xv, in1=xu_tmp[:, 0:n], op=ALU.add, deps=[t1])

        class StatAcc:
            def __init__(self, sqr):
                self.sqr = sqr
                rb = banks.reserve(len(blocks))
                self.banks = {blk["idx"]: rb[i] for i, blk in enumerate(blocks)}
                self.pend = None
                self.tok = {}

            def push(self, m, blk, tu):
                c0, n = blk["c0"], blk["n"]
                sq = self.sqr.get()
                e = op(act, "activation", out=sq.ap[:, 0:n], in_=xT[:, m, c0:c0 + n], func=AF.Square, deps=[tu, sq.free])
                self.flush()
                self.pend = (m, blk, sq, e)

            def flush(self):
                if self.pend is None:
                    return
                m, blk, sq, e = self.pend
                n, bi = blk["n"], blk["idx"]
                bk = self.banks[bi]
                tk = op(pe, "matmul", bk.ap[:, 0:n], lhsT=ones_bf, rhs=sq.ap[:, 0:n], start=(m == 0), stop=(m == 7),
                        deps=[e, bk.free, t_ones] if m == 0 else [e])
                sq.free = TS(tk)
                self.tok[bi] = tk
                self.pend = None

        hw = rmsnorm(0, 1)
        stage[0] = STAGE0
        uT = tnew([128, NCH, SBW], BF16)
        R1 = stage[0]
        vB = tnew([128, 9, D], BF16)
        stage[0] = R1
        gluP = tnew([128, NCH, 30 + 1024], BF16)
        gluS = tnew([128, NCH, NS, 38], BF16)
        R1end = stage[0]
        diags = [Res(tnew([128, 31, 128], BF16)) for _ in range(2)]
        TMP0 = stage[0]
        vraw = [Res(tnew([128, D], F32)) for _ in range(2)]
        lns = [Res((tnew([128, 2, 6], F32), tnew([128, 2], F32), tnew([128, 1], F32), tnew([128, 1], F32), tnew([128, 1], F32),
                    tnew([128, 1], F32))) for _ in range(2)]
        sfree = TS(state["stage_free"])

        P.tag = 'U'
        u_done = {}
        wiU0, woffU0, wtokU0 = ring.get("U")
        wi0, woff0, wtok0 = ring.get("V")
        wi1, woff1, wtok1 = ring.get("V")
        wiU1, woffU1, wtokU1 = ring.get("U")
        u_list = []

        def mk_u(p, jj, blk, W, wtok, last, wi):
            def f():
                j = p * 4 + jj
                c0, n, bi = blk["c0"], blk["n"], blk["idx"]
                bk = banks.get()
                for k in range(8):
                    tk = op(pe, "matmul", bk.ap[:, 0:n], lhsT=W[:, k, jj * 128:(jj + 1) * 128], rhs=hT[:, k, c0:c0 + n],
                            start=(k == 0), stop=(k == 7), deps=[wtok, hw[bi], bk.free] if k == 0 else (), mark=(k == 7))
                tg = gelu(bk.ap[:, 0:n], uT[:, j, c0:c0 + n], n, [tk, sfree])
                bk.free = TS(tg)
                u_done[bi] = TS(tg)
                if last:
                    ring.release(wi, tk)
            return f

        for p, (wi_, woff_, wtok_) in enumerate(((wiU0, woffU0, wtokU0), (wiU1, woffU1, wtokU1))):
            W = view(woff_, [128, 8, 512], BF16)
            pairs = [(jj, blk) for blk in blocks for jj in range(4)] if p == 0 else [(jj, blk) for jj in range(4) for blk in blocks]
            for i_, (jj, blk) in enumerate(pairs):
                u_list.append(mk_u(p, jj, blk, W, wtok_, i_ == len(pairs) - 1, wi_))

        P.tag = 'V'
        Wv = [view(woff0, [128, 8, 512], BF16), view(woff1, [128, 8, 512], BF16)]
        v_done = {}
        vst = {}

        def v_ph1(ti):
            bi = min(ti // 4, 2)
            vr = vraw[ti % 2]
            st_ = lns[ti % 2]
            tg = None
            for hh in range(2):
                bk = banks.get()
                for k in range(8):
                    tk = op(pe, "matmul", bk.ap, lhsT=hT[:, k, ti * 128:(ti + 1) * 128], rhs=Wv[hh][:, k, :], start=(k == 0),
                            stop=(k == 7), deps=[wtok0, wtok1, hw[bi], bk.free] if k == 0 else (), mark=(k == 7))
                tg = gelu(bk.ap, vr.ap[:, hh * 512:(hh + 1) * 512], 512, [tk, sfree], out_free=vr.free)
                bk.free = TS(tg)
            lnst, lnmv, lna, lny, lnt, nmr = st_.ap
            t1 = op(dve, "bn_stats", out=lnst[:, 0, :], in_=vr.ap[:, 0:512], deps=[tg, st_.free])
            t1 = op(dve, "bn_stats", out=lnst[:, 1, :], in_=vr.ap[:, 512:1024], deps=[tg])
            t1 = op(dve, "bn_aggr", out=lnmv, in_=lnst.rearrange("p a b -> p (a b)"), deps=[t1])
            t1 = op(dve, "tensor_scalar", out=lna, in0=lnmv[:, 1:2], scalar1=EPS, scalar2=None, op0=ALU.add, deps=[t1])
            t1 = newton(lna, lny, lnt, 2, [t1])
            t1 = op(dve, "scalar_tensor_tensor", out=nmr, in0=lnmv[:, 0:1], scalar=-1.0, in1=lny, op0=ALU.mult, op1=ALU.mult, deps=[t1])
            vst[ti] = (vr, st_, t1, tk)

        def v_ph2(ti):
            vr, st_, t1, tk = vst.pop(ti)
            lnst, lnmv, lna, lny, lnt, nmr = st_.ap
            e1 = op(act, "activation", out=vr.ap, in_=vr.ap, func=AF.Identity, bias=nmr, scale=lny, deps=[t1])
            st_.free = TS(e1)
            t1 = op(dve, "tensor_tensor", out=vr.ap, in0=vr.ap, in1=lnbc[:, 0, :], op=ALU.mult, deps=[e1, t_const])
            if ti < 8:
                t2 = op(dve, "tensor_tensor", out=vB[:, ti, :], in0=vr.ap, in1=lnbc[:, 1, :], op=ALU.add, deps=[t1, sfree])
                vr.free = TS(t2)
            else:
                t1 = op(dve, "tensor_tensor", out=vr.ap, in0=vr.ap, in1=lnbc[:, 1, :], op=ALU.add, deps=[t1])
                t2 = op(act, "activation", out=vB[:, ti, :], in_=vr.ap, func=AF.Copy, deps=[t1, sfree])
                t3 = sp.dma(s_nv, nv_s, vr.ap, deps=[t1])
                out_toks.add(t3)
                vr.free = TS(t2, t3)
            v_done[ti] = t2

        n_u = len(u_list)
        per = -(-n_u // ntile)
        ui = 0
        for step in range(ntile + 1):
            if step < ntile:
                v_ph1(step)
            if step >= 1:
                v_ph2(step - 1)
            for _ in range(per):
                if ui < n_u:
                    u_list[ui]()
                    ui += 1
        while ui < n_u:
            u_list[ui]()
            ui += 1
        tk = Tok("pe", None, pe.cnt)
        ring.release(wi0, tk)
        ring.release(wi1, tk)

        P.tag = 'SGU'
        a_done = {}
        for blk in blocks:
            c0, n, bi = blk["c0"], blk["n"], blk["idx"]
            tiles = [c0 // 128 + i for i in range(n // 128)]
            WT = WT_p if blk["kind"] == "p" else WT_s
            bvar = 0 if blk["kind"] == "p" else 1
            for h in range(8):
                bk = banks.get()
                for i, ti in enumerate(tiles):
                    op(pe, "matmul", bk.ap[:, i * 128:(i + 1) * 128], lhsT=vB[:, ti, h * 128:(h + 1) * 128], rhs=WT[:, h, :],
                       start=True, stop=False, deps=[v_done[ti], bk.free, t_WT] if i == 0 else [v_done[ti]], mark=False)
                    tk = op(pe, "matmul", bk.ap[:, i * 128:(i + 1) * 128], lhsT=sel16[:, h, :], rhs=bias16[:, bvar, :],
                            start=False, stop=True, deps=[t_sel], mark=(i == len(tiles) - 1))
                ta = op(dve, "tensor_tensor", out=uT[:, h, c0:c0 + n], in0=uT[:, h, c0:c0 + n], in1=bk.ap[:, 0:n], op=ALU.mult,
                        deps=[tk, u_done[bi]])
                bk.free = TS(ta)
                a_done[bi] = TS(ta)
        v_dead = TS(Tok("pe", None, pe.cnt))

        P.tag = 'Bhist'
        stage[0] = TMP0
        CVF0 = (stage[0] + 63) // 64 * 64
        cvf = Rot([Res(tnew([128, 512], F32)) for _ in range(3)])
        CVB0 = (stage[0] + 63) // 64 * 64
        cvb = Rot([Res(tnew([128, 512], BF16)) for _ in range(2)])
        sqb = Rot([Res(tnew([128, 512], BF16)) for _ in range(2)])
        t1_t = tnew([128, 512], F32); a_t = tnew([128, 512], F32); y_t = tnew([128, 512], F32); nt_t = t1_t
        sgt = Rot([Res(t1_t), Res(a_t)])
        stbs = Res(view(CVF0, [128, D], F32))
        bfree = TS(v_dead, Tok("dve", None, dve.cnt), Tok("act", None, act.cnt))
        hist_jobs = []
        if sbi == 0:
            t_hist = op(dve, "memset", gluP[:, :, 0:30], 0.0, deps=[bfree])
        else:
            t_hist = op(dve, "tensor_copy", out=gluP[:, :, 0:30], in_=glu_carry, deps=[bfree])
            stb2 = [stbs, Res(view(CVB0, [128, D], F32))]
            s_stb2 = [s_stb, s_stb_b]
            hist_dma = {}

            def hist_load(rt):
                if rt < 4 and rt not in hist_dma:
                    b_ = stb2[rt % 2]
                    hist_dma[rt] = sp.dma(s_stb2[rt % 2], b_.ap, st_b[rt * 128:(rt + 1) * 128, :], deps=[b_.free, bfree])

            def mk_hist(rt):
                def f():
                    hist_load(rt)
                    hist_load(rt + 1)
                    b_ = stb2[rt % 2]
                    tl = hist_dma[rt]
                    for g in range(2):
                        bk = banks.get()
                        tk = transposes_to(bk, [b_.ap[:, (g * 4 + i) * 128:(g * 4 + i + 1) * 128] for i in range(4)], 128, [tl])
                        th = op(act, "activation", out=gluS[:, g * 4:(g + 1) * 4, rt * 4:(rt + 1) * 4, 0:30],
                                in_=bk.ap.rearrange("p (a b c) -> p a b c", a=4, b=4)[:, :, :, 0:30], func=AF.Copy,
                                deps=[tk, bfree])
                        bk.free = TS(th)
                        glu_done.add(th)
                    b_.free = TS(tk)
                return f
            hist_load(0)
            hist_load(1)
            hist_jobs = [mk_hist(rt) for rt in range(4)]
            out_toks.add(sp.dma(s_outmisc, nb_s[:, 0:22, :], st_b_raw[:, 8:30, :]))

        P.tag = 'AG'
        glu_done = TS(t_hist)
        for p in range(4):
            wi, woff, wtok = ring.get("AG")
            W = view(woff, [128, 8, 2, 256], BF16)
            if hist_jobs:
                hist_jobs.pop(0)()
            for jj in range(2):
                j = p * 2 + jj
                for blk in blocks:
                    c0, n, bi = blk["c0"], blk["n"], blk["idx"]
                    bka = banks.get(); bkg = banks.get()
                    for (bk, ci) in ((bka, 0), (bkg, 1)):
                        for k in range(8):
                            tk = op(pe, "matmul", bk.ap[:, 0:n], lhsT=W[:, k, ci, jj * 128:(jj + 1) * 128], rhs=hT[:, k, c0:c0 + n],
                                    start=(k == 0), stop=(k == 7), deps=[wtok, hw[bi], bk.free] if k == 0 else (), mark=(k == 7))
                    sg = sgt.get()
                    t1 = op(act, "activation", out=sg.ap[:, 0:n], in_=bkg.ap[:, 0:n], func=AF.Sigmoid, deps=[tk, sg.free, bfree])
                    bkg.free = TS(t1)
                    if blk["kind"] == "p":
                        t2 = op(dve, "tensor_tensor", out=gluP[:, j, 30 + c0:30 + c0 + n], in0=bka.ap[:, 0:n], in1=sg.ap[:, 0:n],
                                op=ALU.mult, deps=[t1, t_hist, bfree])
                        if sbi == 1 and bi == 1:
                            t2 = op(dve, "tensor_tensor", out=tailb_p[:, j, 0:30], in0=bka.ap[:, n - 30:n], in1=sg.ap[:, n - 30:n],
                                    op=ALU.mult, deps=[t1])
                    else:
                        t2 = op(dve, "tensor_tensor", out=tailb_s[:, j, :], in0=bka.ap[:, 0:n], in1=sg.ap[:, 0:n], op=ALU.mult,
                                deps=[t1])
                        t2 = op(dve, "tensor_copy", out=gluS[:, j, :, 30:38], in_=q3(tailb_s[:, j, :]), deps=[t2, t_hist, bfree])
                    sg.free = TS(t2)
                    bka.free = TS(t2)
                    glu_done.add(t2)
            ring.release(wi, tk)
        h_dead = TS(Tok("pe", None, pe.cnt))
        bT = hT

        P.tag = 'conv'
        b_done = {}
        b_tok = {}
        its = [(j, blk) for j in range(8) for blk in blocks]
        stA = {}; stB = {}
        a_free = [TS()]
        tdg_of = {}

        def build_diag(j_):
            if j_ >= 8 or j_ in tdg_of:
                return
            tdg_of[j_] = op(pool, "tensor_tensor", out=diags[j_ % 2].ap, in0=identc.unsqueeze(1).to_broadcast([128, 31, 128]),
                            in1=colv1[:, j_, 5:36].unsqueeze(2).to_broadcast([128, 31, 128]), op=ALU.mult,
                            deps=[diags[j_ % 2].free, t_colv1, t_const, sfree])

        def phA(i):
            j, blk = its[i]
            c0, n, bi = blk["c0"], blk["n"], blk["idx"]
            if blk is blocks[0]:
                build_diag(j)
                build_diag(j + 1)
            dg = diags[j % 2]
            bk = banks.get()
            for k in range(31):
                rhs = gluP[:, j, c0 + k:c0 + k + n] if blk["kind"] == "p" else gluS[:, j, :, k:k + 8]
                tk = op(pe, "matmul", bk.ap[:, 0:n], lhsT=dg.ap[:, k, :], rhs=rhs, start=(k == 0), stop=(k == 30),
                        deps=[tdg_of[j], glu_done, bk.free] if k == 0 else (), mark=(k == 30))
            if blk is blocks[-1]:
                dg.free = TS(tk)
            cb = colv1[:, j, 36:37]
            cf = cvf.get(); sq = sqb.get()
            e1 = op(act, "activation", out=cf.ap[:, 0:n], in_=bk.ap[:, 0:n], func=AF.Identity, bias=cb, scale=1.0,
                    deps=[tk, cf.free, bfree])
            e3 = op(act, "activation", out=sq.ap[:, 0:n], in_=bk.ap[:, 0:n], func=AF.Square, bias=cb, scale=1.0, deps=[sq.free])
            bk.free = TS(e3)
            stA[i] = (cf, sq, e1, e3)

        def phB(i):
            j, blk = its[i]
            c0, n, bi = blk["c0"], blk["n"], blk["idx"]
            cf, sq, e1, e3 = stA.pop(i)
            bk2 = banks.get()
            m2 = op(pe, "matmul", bk2.ap[:, 0:n], lhsT=ones128, rhs=sq.ap[:, 0:n], start=True, stop=True, deps=[e3, bk2.free, t_ones])
            sq.free = TS(m2)
            d3 = op(act, "activation", out=a_t[:, 0:n], in_=bk2.ap[:, 0:n], func=AF.Identity, bias=epsc[:, 0:1], scale=1.0,
                    deps=[m2, a_free[0], bfree])
            bk2.free = TS(d3)
            ty = newton(a_t[:, 0:n], y_t[:, 0:n], nt_t[:, 0:n], 2, [d3])
            d6 = op(dve, "tensor_tensor", out=cf.ap[:, 0:n], in0=cf.ap[:, 0:n], in1=y_t[:, 0:n], op=ALU.mult, deps=[ty, e1])
            a_free[0] = TS(ty)
            stB[i] = (cf, d6)

        def phC(i):
            j, blk = its[i]
            c0, n, bi = blk["c0"], blk["n"], blk["idx"]
            cf, d6 = stB.pop(i)
            g_ = colv1[:, j, 37:38]; b_ = colv1[:, j, 38:39]
            e5 = op(act, "activation", out=bT[:, j, c0:c0 + n], in_=cf.ap[:, 0:n], func=AF.Silu, bias=b_, scale=g_, deps=[d6, h_dead])
            cf.free = TS(e5)
            b_done[bi] = TS(e5)
            b_tok[(j, bi)] = e5

        mtok1 = [Res(view_of(tailb_s, 0, [17, 512])), Res(view_of(tailb_s, 4, [17, 512]))] if sbi == 0 else None
        n_ada1 = 0
        ada_p2 = []
        for step in range(len(its) + 2):
            if step < len(its):
                phA(step)
            if 0 <= step - 1 < len(its):
                phB(step - 1)
            if 0 <= step - 2 < len(its):
                phC(step - 2)
            if sbi == 0 and step >= 2:
                if ada_p2:
                    state["t_mod1"] = ada_p2.pop()()
                    if n_ada1 == 12:
                        state["t_mod1"] = ada_finish(1, state["t_mod1"])
                if n_ada1 < 12:
                    ada_p2.append(ada_step(1, n_ada1, mtok1[n_ada1 % 2], defer=True))
                    n_ada1 += 1
        assert not (sbi == 0 and (ada_p2 or n_ada1 != 12))
        if sbi == 0:
            op(act, "activation", out=glu_carry, in_=gluP[:, :, 1024:1054], func=AF.Copy, deps=[glu_done])

        P.tag = 'WO'
        sacc = StatAcc(Rot([Res(view(CVB0 + 1024 * i, [128, 512], BF16)) for i in range(3)]))
        for r_ in sacc.sqr.items:
            r_.free = TS(Tok("pe", None, pe.cnt), Tok("act", None, act.cnt))
        state["pre"] = sacc
        def wo_mm(W, wtok, mm, blk, bk, klo, khi):
            c0, n, bi = blk["c0"], blk["n"], blk["idx"]
            tk_ = None
            for k in range(klo, khi):
                rhs = uT[:, k, c0:c0 + n] if k < 8 else bT[:, k - 8, c0:c0 + n]
                tk_ = op(pe, "matmul", bk.ap[:, 0:n], lhsT=W[:, k, mm * 128:(mm + 1) * 128], rhs=rhs, start=(k == 0),
                         stop=(k == 15), deps=[wtok, a_done[bi], bk.free] if k == 0 else ([b_tok[(k - 8, bi)]] if k >= 8 else ()),
                         mark=(k == 15))
            return tk_

        def wo_fin(bk, m, blk, tk_):
            n, bi = blk["n"], blk["idx"]
            tu = x_update(bk.ap[:, 0:n], m, blk, 0, "g1", [tk_])
            bk.free = TS(tu)
            x_ready[bi] = TS(tu)
            sacc.push(m, blk, tu)

        def wo_group(W, wtok, m, mm, blk):
            bk = banks.get()
            tk_ = wo_mm(W, wtok, mm, blk, bk, 0, 16)
            wo_fin(bk, m, blk, tk_)
            return tk_

        for p in range(2):
            wi, woff, wtok = ring.get("WO")
            W = view(woff, [128, 16, 256], BF16)
            for mm in range(2):
                if p == 0 and mm == 0:
                    ob = {blk["idx"]: banks.get() for blk in blocks}
                    for blk in blocks:
                        wo_mm(W, wtok, mm, blk, ob[blk["idx"]], 0, 14)
                    for blk in blocks:
                        tk = wo_mm(W, wtok, mm, blk, ob[blk["idx"]], 14, 16)
                        wo_fin(ob[blk["idx"]], 0, blk, tk)
                    continue
                for blk in blocks:
                    tk = wo_group(W, wtok, p * 2 + mm, mm, blk)
            ring.release(wi, tk)
        wo2 = [ring.get("WO") for _ in range(2)]
        Wo2 = [view(w_[1], [128, 16, 256], BF16) for w_ in wo2]
        nT = early_T([cvf.items[0].ap, cvf.items[1].ap, cvf.items[2].ap, t1_t, a_t, y_t])
        n_sfree = TS(Tok("pe", None, pe.cnt), Tok("dve", None, dve.cnt), Tok("act", None, act.cnt))
        hw_e = {}
        pend_blk = None
        pend_hfree = None
        stp = None
        for blk in blocks:
            for m in range(4, 8):
                p2, mm = divmod(m - 4, 2)
                tk = wo_group(Wo2[p2], wo2[p2][2], m, mm, blk)
                if m == 4 and pend_blk is not None:
                    stp = Stepper(norm_gen(pend_blk, 0, 2, False, sacc, nT, n_sfree, pend_hfree), hw_e, pend_blk["idx"])
                    pend_blk = None
                if stp is not None:
                    stp.step(4)
            if stp is not None:
                stp.finish()
                stp = None
            pend_blk = blk
            pend_hfree = TS(tk)
        sacc.flush()
        hw_e[pend_blk["idx"]] = norm_block(pend_blk, 0, 2, False, sacc, nT, n_sfree, pend_hfree)
        for w_ in wo2:
            ring.release(w_[0], tk)
        state["hw_pre"] = hw_e
        state["h_rd"] = TS(Tok("pe", None, pe.cnt))
        state["stage_free"] = TS(Tok("pe", None, pe.cnt), Tok("dve", None, dve.cnt), Tok("act", None, act.cnt))

        def ffn(L):
            hw = rmsnorm(L, 2)
            P.tag = 'UP'
            stage[0] = STAGE0
            actT = tnew([128, 12, SBW], BF16)
            upP = [[Res(tnew([128, 2 + 1024], F32)) for _ in range(2)] for _ in range(2)]
            upS = [[Res(tnew([128, NS, 10], F32)) for _ in range(2)] for _ in range(2)]
            accs = Rot([(Res(tnew([128, 512], F32)), Res(tnew([128, 512], F32))) for _ in range(3)])
            sgf = Rot([Res(tnew([128, 512], F32)) for _ in range(2)])
            sqr_f = Rot([Res(tnew([128, 512], BF16)) for _ in range(3)])
            sfree = TS(state["stage_free"])
            rot = 0
            pend_gate = []
            flat_c = [k0 + 2 * p + jj for (k0, nk) in ((0, 12), (12, 10)) for p in range(nk // 2) for jj in range(2)]
            hist_toks = {}

            def init_hist(i):
                if i >= len(flat_c) or i in hist_toks:
                    return
                c_ = flat_c[i]
                r_ = i % 2
                toks = []
                for gv in range(2):
                    cj = gv * NFC + c_
                    u_ = upP[gv][r_]; us_ = upS[gv][r_]
                    if sbi == 0:
                        th = op(pool, "memset", u_.ap[:, 0:2], 0.0, deps=[u_.free, sfree])
                    else:
                        th = op(pool, "tensor_copy", out=u_.ap[:, 0:2], in_=f_carry[:, L, cj, :], deps=[u_.free, sfree])
                        th = op(pool, "tensor_copy", out=us_.ap[:, :, 0:2],
                                in_=sfT[:, L, cj, :].rearrange("p (q r) -> p q r", q=NS), deps=[us_.free, sfree, t_sfT])
                    toks.append(th)
                hist_toks[i] = toks

            for half in range(2):
                k0, nk = (0, 12) if half == 0 else (12, 10)
                act_done = {}
                act_tok = {}
                act_free = TS(Tok("pe", None, pe.cnt)) if half == 1 else sfree
                P.tag = 'UP'
                for p in range(nk // 2):
                    wi, woff, wtok = ring.get("UP")
                    W = view(woff, [128, 8, 2, 256], BF16)
                    chs = []
                    for jj in range(2):
                        c = k0 + 2 * p + jj
                        cl = 2 * p + jj
                        r = rot % 2
                        rot += 1
                        ups = [upP[0][r], upP[1][r]]
                        upss = [upS[0][r], upS[1][r]]
                        init_hist(rot - 1)
                        hist = hist_toks[rot - 1]
                        chs.append((jj, c, cl, ups, upss, hist, rot - 1))
                    first_piece = (half == 0 and p == 0)
                    pairs = [(ch, blk) for blk in blocks for ch in chs] if first_piece else [(ch, blk) for ch in chs for blk in blocks]
                    for (jj, c, cl, ups, upss, hist, cidx), blk in pairs:
                        c0, n, bi = blk["c0"], blk["n"], blk["idx"]
                        ac = accs.get()
                        tacc = []
                        for gv in range(2):
                            cj = gv * NFC + c
                            bk = banks.get()
                            for k in range(8):
                                tk = op(pe, "matmul", bk.ap[:, 0:n], lhsT=W[:, k, gv, jj * 128:(jj + 1) * 128],
                                        rhs=hT[:, k, c0:c0 + n], start=(k == 0), stop=(k == 7),
                                        deps=[wtok, hw[bi], bk.free] if k == 0 else (), mark=(k == 7))
                            w0 = colv2[:, cj, L * 3 + 0:L * 3 + 1]; w1 = colv2[:, cj, L * 3 + 1:L * 3 + 2]
                            w2 = colv2[:, cj, L * 3 + 2:L * 3 + 3]; cb = colv2[:, cj, 6 + L:7 + L]
                            if blk["kind"] == "p":
                                raw = ups[gv].ap
                                cur_v = raw[:, 2 + c0:2 + c0 + n]; m1 = raw[:, 1 + c0:1 + c0 + n]; m2 = raw[:, c0:c0 + n]
                                psv = bk.ap[:, 0:n]; accv = ac[gv].ap[:, 0:n]
                            else:
                                raw = upss[gv].ap
                                cur_v = raw[:, :, 2:10]; m1 = raw[:, :, 1:9]; m2 = raw[:, :, 0:8]
                                psv = q3(bk.ap[:, 0:n]); accv = q3(ac[gv].ap[:, 0:n])
                            e1 = op(act, "activation", out=cur_v, in_=psv, func=AF.Copy, deps=[tk, hist[gv], t_colv2, sfree])
                            e2 = op(act, "activation", out=accv, in_=psv, func=AF.Identity, bias=cb, scale=w2, deps=[ac[gv].free])
                            bk.free = TS(e2)
                            d1 = op(dve, "scalar_tensor_tensor", out=accv, in0=m1, scalar=w1, in1=accv, op0=ALU.mult, op1=ALU.add,
                                    deps=[e1, e2])
                            d2 = op(dve, "scalar_tensor_tensor", out=accv, in0=m2, scalar=w0, in1=accv, op0=ALU.mult, op1=ALU.add,
                                    deps=[d1])
                            tacc.append(d2)
                            if blk["kind"] == "p" and bi == 1:
                                ups[gv].free = TS(d2)
                                ups[gv].free = TS(op(pool, "tensor_copy", out=f_carry[:, L, cj, :], in_=raw[:, 1024:1026], deps=[d2]))
                            if blk["kind"] == "s":
                                upss[gv].free = TS(op(pool, "tensor_copy", out=sfT[:, L, cj, :].rearrange("p (q r) -> p q r", q=NS),
                                                      in_=raw[:, :, 8:10], deps=[d2]))
                        def gate(ac=ac, tacc=tacc, cl=cl, c0=c0, n=n, bi=bi, act_free=act_free, act_done=act_done, act_tok=act_tok):
                            sg = sgf.get()
                            e3 = op(act, "activation", out=sg.ap[:, 0:n], in_=ac[0].ap[:, 0:n], func=AF.Silu, deps=[tacc[0], sg.free])
                            d4 = op(pool if n == 128 else dve, "tensor_tensor", out=actT[:, cl, c0:c0 + n], in0=sg.ap[:, 0:n],
                                    in1=ac[1].ap[:, 0:n], op=ALU.mult, deps=[e3, tacc[1], act_free])
                            sg.free = TS(d4); ac[0].free = TS(d4); ac[1].free = TS(d4)
                            act_done[bi] = TS(d4)
                            act_tok[(cl, bi)] = d4
                        if pend_gate:
                            pend_gate.pop()()
                        pend_gate.append(gate)
                        if blk is blocks[-1]:
                            init_hist(cidx + 2)
                    ring.release(wi, tk)
                if pend_gate:
                    pend_gate.pop()()
                P.tag = 'DN'
                if half == 1:
                    sacc = StatAcc(sqr_f)
                    state["pre"] = sacc
                if half == 1:
                    nL, nW, nF = (1, 1, False) if L == 0 else (0, 1, True)
                    dn = []
                    for (m0, cnt) in ((0, 3), (3, 3), (6, 2)):
                        wi_, woff_, wtok_ = ring.get("DN")
                        dn.append((wi_, view(woff_, [128, nk, cnt * 128], BF16), wtok_, m0, cnt))
                    tl_ = [upP[0][1].ap[:, 0:512], upP[0][1].ap[:, 512:1024], upP[1][0].ap[:, 0:512], upP[1][0].ap[:, 512:1024],
                           upP[1][1].ap[:, 0:512], upP[1][1].ap[:, 512:1024]]
                    if nF:
                        nT = dict(sq=None, av=[Res(tl_[0])], yv=[Res(tl_[1]), Res(tl_[2]), Res(tl_[3])], tv=tl_[4], tks=None)
                    else:
                        nT = early_T(tl_)
                    n_sfree = TS(Tok("dve", None, dve.cnt), Tok("act", None, act.cnt), Tok("pool", None, pool.cnt))
                    n_hfree = TS(Tok("pe", None, pe.cnt))
                    hw_e = {}
                    pend_blk = None
                    stp = None
                    def dn_mm(blk, m, bk, klo, khi):
                        c0, n, bi = blk["c0"], blk["n"], blk["idx"]
                        wi_, Wd, wtok_, m0, cnt = next(d_ for d_ in dn if d_[3] <= m < d_[3] + d_[4])
                        tk_ = None
                        for k in range(klo, khi):
                            tk_ = op(pe, "matmul", bk.ap[:, 0:n], lhsT=Wd[:, k, (m - m0) * 128:(m - m0 + 1) * 128],
                                     rhs=actT[:, k, c0:c0 + n], start=(k == 0), stop=(k == nk - 1),
                                     deps=[wtok_, bk.free, act_tok[(k, bi)]] if k == 0 else [act_tok[(k, bi)]], mark=(k == nk - 1))
                        return tk_

                    for blk in blocks:
                        c0, n, bi = blk["c0"], blk["n"], blk["idx"]
                        opened = {}
                        if blk is blocks[0]:
                            for m in range(3):
                                opened[m] = banks.get()
                                dn_mm(blk, m, opened[m], 0, nk - 3)
                        for m in range(8):
                            if m in opened:
                                bk = opened[m]
                                tk = dn_mm(blk, m, bk, nk - 3, nk)
                            else:
                                bk = banks.get()
                                tk = dn_mm(blk, m, bk, 0, nk)
                            tu = x_update(bk.ap[:, 0:n], m, blk, L, "g2", [tk])
                            bk.free = TS(tu)
                            x_ready[bi] = TS(tu)
                            sacc.push(m, blk, tu)
                            if m == 0 and pend_blk is not None:
                                stp = Stepper(norm_gen(pend_blk, nL, nW, nF, sacc, nT, n_sfree, n_hfree), hw_e, pend_blk["idx"])
                                pend_blk = None
                            if stp is not None:
                                stp.step(2)
                        if stp is not None:
                            stp.finish()
                            stp = None
                        pend_blk = blk
                    sacc.flush()
                    hw_e[pend_blk["idx"]] = norm_block(pend_blk, nL, nW, nF, sacc, nT, n_sfree, n_hfree)
                    for d_ in dn:
                        ring.release(d_[0], tk)
                    state["hw_pre"] = hw_e
                else:
                    for m in range(8):
                        wi, woff, wtok = ring.get("DN")
                        W = view(woff, [128, nk, 128], BF16)
                        ksplit = nk - 3 if m == 0 else nk
                        bks_ = {}
                        for blk in blocks:
                            c0, n, bi = blk["c0"], blk["n"], blk["idx"]
                            bk = banks.get()
                            bks_[bi] = bk
                            for k in range(ksplit):
                                tk = op(pe, "matmul", bk.ap[:, 0:n], lhsT=W[:, k, :], rhs=actT[:, k, c0:c0 + n], start=(k == 0),
                                        stop=(k == nk - 1), deps=[wtok, bk.free, act_tok[(k, bi)]] if k == 0 else [act_tok[(k, bi)]],
                                        mark=(k == nk - 1))
                            if ksplit == nk:
                                tu = x_update(bk.ap[:, 0:n], m, blk, L, "g2", [tk])
                                bk.free = TS(tu)
                                x_ready[bi] = TS(tu)
                                if half == 1:
                                    sacc.push(m, blk, tu)
                        if ksplit < nk:
                            for blk in blocks:
                                c0, n, bi = blk["c0"], blk["n"], blk["idx"]
                                bk = bks_[bi]
                                for k in range(ksplit, nk):
                                    tk = op(pe, "matmul", bk.ap[:, 0:n], lhsT=W[:, k, :], rhs=actT[:, k, c0:c0 + n], start=False,
                                            stop=(k == nk - 1), deps=[act_tok[(k, bi)]], mark=(k == nk - 1))
                                tu = x_update(bk.ap[:, 0:n], m, blk, L, "g2", [tk])
                                bk.free = TS(tu)
                                x_ready[bi] = TS(tu)
                                if half == 1:
                                    sacc.push(m, blk, tu)
                        ring.release(wi, tk)
            state["h_rd"] = TS(Tok("pe", None, pe.cnt))
            state["stage_free"] = TS(Tok("pe", None, pe.cnt), Tok("dve", None, dve.cnt), Tok("act", None, act.cnt))

        ffn(0)

        hw = rmsnorm(1, 1)
        P.tag = 'C'
        stage[0] = STAGE0
        gT = tnew([128, NCH, SBW], BF16)
        prP = [Res(tnew([128, 2 + 1024], F32)) for _ in range(2)]
        prS = [Res(tnew([128, NS, 10], F32)) for _ in range(2)]
        hxt = Rot([Res(tnew([128, 512], F32)) for _ in range(2)])
        cac = Rot([Res(tnew([128, 512], F32)) for _ in range(2)])
        sqr_c = Rot([Res(tnew([128, 512], BF16)) for _ in range(3)])
        sfree = TS(state["stage_free"])
        g_done = {}
        g_tok = {}
        histc = {}

        def init_hist_c(j_):
            if j_ >= 8 or j_ in histc:
                return
            pr_ = prP[j_ % 2]; prs_ = prS[j_ % 2]
            if sbi == 0:
                th_ = op(pool, "memset", pr_.ap[:, 0:2], 0.0, deps=[pr_.free, sfree])
            else:
                th_ = op(pool, "tensor_copy", out=pr_.ap[:, 0:2], in_=c_carry[:, j_, :], deps=[pr_.free, sfree])
                th_ = op(pool, "tensor_copy", out=prs_.ap[:, :, 0:2], in_=scT[:, j_, :].rearrange("p (q r) -> p q r", q=NS),
                         deps=[prs_.free, sfree, t_scT])
            histc[j_] = th_

        for j in range(8):
            wi, woff, wtok = ring.get("C")
            W = view(woff, [128, 8, 3, 128], BF16)
            pr = prP[j % 2]; prs = prS[j % 2]
            init_hist_c(j)
            init_hist_c(j + 1)
            th = histc[j]
            w0 = colv1[:, j, 39:40]; w1 = colv1[:, j, 40:41]; w2 = colv1[:, j, 41:42]
            for blk in blocks:
                c0, n, bi = blk["c0"], blk["n"], blk["idx"]
                bks = [banks.get() for _ in range(3)]
                for i in range(3):
                    for k in range(8):
                        tk = op(pe, "matmul", bks[i].ap[:, 0:n], lhsT=W[:, k, i, :], rhs=hT[:, k, c0:c0 + n], start=(k == 0),
                                stop=(k == 7), deps=[wtok, hw[bi], bks[i].free] if k == 0 else (), mark=(k == 7))
                hx = hxt.get(); ca = cac.get()
                if blk["kind"] == "p":
                    raw = pr.ap
                    cur_v = raw[:, 2 + c0:2 + c0 + n]; m1 = raw[:, 1 + c0:1 + c0 + n]; m2 = raw[:, c0:c0 + n]
                    f = lambda a: a
                else:
                    raw = prs.ap
                    cur_v = raw[:, :, 2:10]; m1 = raw[:, :, 1:9]; m2 = raw[:, :, 0:8]
                    f = q3
                e1 = op(act, "activation", out=hx.ap[:, 0:n], in_=bks[2].ap[:, 0:n], func=AF.Copy, deps=[tk, hx.free, sfree])
                bks[2].free = TS(e1)
                d1 = op(dve, "tensor_tensor", out=cur_v, in0=f(bks[1].ap[:, 0:n]), in1=f(hx.ap[:, 0:n]), op=ALU.mult, deps=[e1, th])
                bks[1].free = TS(d1); hx.free = TS(d1)
                d2 = op(dve, "tensor_scalar", out=f(ca.ap[:, 0:n]), in0=cur_v, scalar1=w2, scalar2=None, op0=ALU.mult,
                        deps=[d1, ca.free, t_colv1])
                d2 = op(dve, "scalar_tensor_tensor", out=f(ca.ap[:, 0:n]), in0=m1, scalar=w1, in1=f(ca.ap[:, 0:n]), op0=ALU.mult,
                        op1=ALU.add, deps=[d2])
                d2 = op(dve, "scalar_tensor_tensor", out=f(ca.ap[:, 0:n]), in0=m2, scalar=w0, in1=f(ca.ap[:, 0:n]), op0=ALU.mult,
                        op1=ALU.add, deps=[d2])
                d3 = op(dve, "tensor_tensor", out=gT[:, j, c0:c0 + n], in0=bks[0].ap[:, 0:n], in1=ca.ap[:, 0:n], op=ALU.mult,
                        deps=[d2, sfree])
                bks[0].free = TS(d3); ca.free = TS(d3)
                g_done[bi] = TS(d3)
                g_tok[(j, bi)] = d3
                if blk["kind"] == "p" and bi == 1:
                    pr.free = TS(op(pool, "tensor_copy", out=c_carry[:, j, :], in_=raw[:, 1024:1026], deps=[d2]))
                if blk["kind"] == "s":
                    prs.free = TS(op(pool, "tensor_copy", out=scT[:, j, :].rearrange("p (q r) -> p q r", q=NS), in_=raw[:, :, 8:10],
                                     deps=[d2]))
            ring.release(wi, tk)
        P.tag = 'WC'
        sacc = StatAcc(sqr_c)
        state["pre"] = sacc
        wc = [ring.get("WC") for _ in range(2)]
        Wc = [view(w_[1], [128, 8, 512], BF16) for w_ in wc]
        nT = early_T([tnew([128, 512], F32) for _ in range(6)])
        n_sfree = TS(sfree)
        n_hfree = TS(Tok("pe", None, pe.cnt))
        hw_e = {}
        pend_blk = None
        stp = None
        def wc_mm(blk, m, bk, klo, khi):
            c0, n, bi = blk["c0"], blk["n"], blk["idx"]
            p, mm = divmod(m, 4)
            tk_ = None
            for k in range(klo, khi):
                tk_ = op(pe, "matmul", bk.ap[:, 0:n], lhsT=Wc[p][:, k, mm * 128:(mm + 1) * 128], rhs=gT[:, k, c0:c0 + n],
                         start=(k == 0), stop=(k == 7),
                         deps=[wc[p][2], bk.free, g_tok[(k, bi)]] if k == 0 else [g_tok[(k, bi)]], mark=(k == 7))
            return tk_

        for blk in blocks:
            c0, n, bi = blk["c0"], blk["n"], blk["idx"]
            opened = {}
            if blk is blocks[0]:
                for m in range(3):
                    opened[m] = banks.get()
                    wc_mm(blk, m, opened[m], 0, 5)
            for m in range(8):
                if m in opened:
                    bk = opened[m]
                    tk = wc_mm(blk, m, bk, 5, 8)
                else:
                    bk = banks.get()
                    tk = wc_mm(blk, m, bk, 0, 8)
                tu = x_update(bk.ap[:, 0:n], m, blk, 1, "g1", [tk])
                bk.free = TS(tu)
                x_ready[bi] = TS(tu)
                sacc.push(m, blk, tu)
                if m == 0 and pend_blk is not None:
                    stp = Stepper(norm_gen(pend_blk, 1, 2, False, sacc, nT, n_sfree, n_hfree), hw_e, pend_blk["idx"])
                    pend_blk = None
                if stp is not None:
                    stp.step(2)
            if stp is not None:
                stp.finish()
                stp = None
            pend_blk = blk
        sacc.flush()
        hw_e[pend_blk["idx"]] = norm_block(pend_blk, 1, 2, False, sacc, nT, n_sfree, n_hfree)
        for w_ in wc:
            ring.release(w_[0], tk)
        state["hw_pre"] = hw_e
        state["h_rd"] = TS(Tok("pe", None, pe.cnt))
        state["stage_free"] = TS(Tok("pe", None, pe.cnt), Tok("dve", None, dve.cnt), Tok("act", None, act.cnt))

        ffn(1)

        rst = rmsnorm(0, 1, final=True)
        P.tag = 'final'
        yt = [Res(tnew([128, NCH, 128], F32)) for _ in range(3)]
        yst = [Res(tnew([128, D], F32)) for _ in range(4)]
        for ti in range(ntile):
            bi = min(ti // 4, 2)
            y, ty = rst[bi]
            yoff = (ti * 128) - blocks[bi]["c0"]
            t = yt[ti % 3]
            d1 = None
            for k_ in range(NCH):
                d1 = op(dve, "scalar_tensor_tensor", out=t.ap[:, k_, :], in0=xT[:, k_, ti * 128:(ti + 1) * 128],
                        scalar=colv1[:, k_, 4:5], in1=y.ap[:, yoff:yoff + 128], op0=ALU.mult, op1=ALU.mult,
                        deps=[ty, t.free] if k_ == 0 else ())
            ys = yst[ti % 4]
            ev = TS()
            for g in range(2):
                bk = banks.get()
                tk = None
                for jx in range(4):
                    k = g * 4 + jx
                    tk = op(pe, "transpose", bk.ap[:, jx * 128:(jx + 1) * 128], t.ap[:, k, :], ident_f,
                            deps=[d1, bk.free] if jx == 0 else (), mark=(jx == 3))
                te = op(act, "activation", out=ys.ap[:, g * 512:(g + 1) * 512], in_=bk.ap, func=AF.Copy, deps=[tk, ys.free])
                bk.free = TS(te)
                ev.add(te)
            t.free = TS(tk)
            dst = y_p[sbi * 1024 + ti * 128: sbi * 1024 + (ti + 1) * 128, :] if ti < 8 else y_s
            to = sp.dma(s_y[ti % 4], dst, ys.ap, deps=[ev])
            ys.free = TS(to)
            out_toks.add(to)
        state["stage_free"] = TS(Tok("pe", None, pe.cnt), Tok("dve", None, dve.cnt), Tok("act", None, act.cnt), out_toks)

    P.tag = 'stateout'
    stage = [XT0]

    def tnew2(shape, dt):
        esz = 2 if dt == BF16 else 4
        nb = int(np.prod(shape[1:])) * esz
        off = (stage[0] + 63) // 64 * 64
        stage[0] = off + nb
        assert stage[0] <= ARENA
        return view(off, shape, dt)

    fin = TS(state["stage_free"])
    o_nbp = tnew2([30, D], F32); o_nbs = tnew2([128, D], F32)
    o_ncp = tnew2([2, D], F32); o_ncs = tnew2([32, D], F32)
    o_nfp = tnew2([2, 2, 2 * DFF], F32)
    o_nfs = tnew2([32, 2, 2 * DFF], F32)

    def back(src_list, rows, dst_aps):
        for g in range(0, len(src_list), 4):
            grp = src_list[g:g + 4]
            bk = banks.get()
            tk = None
            for i, s in enumerate(grp):
                tk = op(pe, "transpose", bk.ap[0:rows, i * 128:(i + 1) * 128], s, ident_f,
                        deps=[fin, bk.free] if i == 0 else (), mark=(i == len(grp) - 1))
            te = op(act, "activation", out=dst_aps[g // 4], in_=bk.ap[0:rows, 0:len(grp) * 128], func=AF.Copy, deps=[tk, fin])
            bk.free = TS(te)
        return te

    te = back([tailb_p[:, j, 0:30] for j in range(8)], 30, [o_nbp[:, g * 512:(g + 1) * 512] for g in range(2)])
    out_toks.add(sp.dma(s_outmisc, nb_p, o_nbp, deps=[te]))
    te = back([tailb_s[:, j, :] for j in range(8)], 128, [o_nbs[:, g * 512:(g + 1) * 512] for g in range(2)])
    for qq in range(NS):
        out_toks.add(sp.dma(s_outmisc, nb_s[qq, 22:30, :], o_nbs[qq * 8:(qq + 1) * 8, :], deps=[te]))
    te = back([c_carry[:, j, :] for j in range(8)], 2, [o_ncp[:, g * 512:(g + 1) * 512] for g in range(2)])
    out_toks.add(sp.dma(s_outmisc, nc_p, o_ncp, deps=[te]))
    te = back([scT[:, j, :] for j in range(8)], 32, [o_ncs[:, g * 512:(g + 1) * 512] for g in range(2)])
    out_toks.add(sp.dma(s_outmisc, nc_s, o_ncs, deps=[te]))
    for L in range(2):
        te = back([f_carry[:, L, cj, :] for cj in range(44)], 2, [o_nfp[:, L, g * 512:min(2 * DFF, (g + 1) * 512)] for g in range(11)])
        out_toks.add(sp.dma(s_outmisc, nf_p[L], o_nfp[:, L, :], deps=[te]))
        te = back([sfT[:, L, cj, :] for cj in range(44)], 32, [o_nfs[:, L, g * 512:min(2 * DFF, (g + 1) * 512)] for g in range(11)])
        out_toks.add(sp.dma(s_outmisc, nf_s[L], o_nfs[:, L, :], deps=[te]))

    sp.add(lambda e: e.nop(), deps=[out_toks], mark=False)
    nc_ = P.finish()
    nc_._prog = P if False else None
    _NC_CACHE['prog'] = P
    return nc_


def _get_nc():
    if "nc" not in _NC_CACHE:
        _NC_CACHE["nc"] = build()
    return _NC_CACHE["nc"]


def kernel(x_prompt, x_sample, c_prompt, c_sample, state_conv_b, state_conv_c, state_ffn,
           w_in_ab, sgu_w, sgu_b, sgu_ln_g, sgu_ln_b, convb_w, convb_b, convb_ln_g, convb_ln_b,
           w_out_ab, w_in_c, convc_w, w_out_c, norm_mix_g, norm_ffn_g, ada_w, ada_b,
           ffn_up, ffn_conv_w, ffn_conv_b, ffn_down, final_g):
    f = lambda a: np.ascontiguousarray(np.asarray(a), dtype=np.float32)
    x_prompt, x_sample, c_prompt, c_sample = f(x_prompt), f(x_sample), f(c_prompt), f(c_sample)
    state_conv_b, state_conv_c, state_ffn = f(state_conv_b), f(state_conv_c), f(state_ffn)
    V1 = np.concatenate([f(norm_mix_g), f(norm_ffn_g), f(final_g)[None], f(convb_w)[0], f(convb_b), f(convb_ln_g),
                         f(convb_ln_b), f(convc_w)[0]], axis=0)
    assert V1.shape == (42, D)
    V2 = np.concatenate([f(ffn_conv_w).reshape(6, 2 * DFF), f(ffn_conv_b)], axis=0)
    V3 = f(ada_b)
    ln_gb = np.concatenate([f(sgu_ln_g), f(sgu_ln_b)], axis=0)
    ident = np.eye(128, dtype=np.float32)
    mask = np.triu(np.ones((128, 128), dtype=np.float32))
    shared = dict(w_in_ab=f(w_in_ab)[0], sgu_w=f(sgu_w)[0], sgu_b=f(sgu_b)[0], ln_gb=ln_gb, w_out_ab=f(w_out_ab)[0],
                  w_in_c=f(w_in_c)[0], w_out_c=f(w_out_c)[0], ada_w=f(ada_w), ffn_up=f(ffn_up), ffn_down=f(ffn_down),
                  V1=V1, V2=V2, V3=V3, ident=ident, mask=mask)
    in_maps = []
    for i in range(8):
        sl = slice(NS * i, NS * (i + 1))
        stb = np.zeros((NS, 32, D), np.float32)
        stb[:, :30] = state_conv_b[0, sl]
        m = dict(shared)
        m.update(xp=x_prompt[i], xs=x_sample[sl].reshape(128, D),
                 c17=np.concatenate([c_prompt[i:i + 1], c_sample[sl]], axis=0),
                 st_b=stb.reshape(512, D), st_b_raw=np.ascontiguousarray(state_conv_b[0, sl]),
                 st_c=state_conv_c[0, sl].reshape(32, D), st_f=state_ffn[:, sl].reshape(2, 32, 2 * DFF))
        in_maps.append(m)
    nc = _get_nc()
    res = run_bass_kernel_spmd(nc, in_maps, core_ids=list(range(8))).results
    g = lambda k: [np.asarray(r[k], dtype=np.float32) for r in res]
    y_prompt = np.stack(g("y_p"), 0)
    y_sample = np.concatenate([a.reshape(NS, TS_, D) for a in g("y_s")], 0)
    nb_p = np.stack(g("nb_p"), 0)[None]
    nc_p = np.stack(g("nc_p"), 0)[None]
    nf_p = np.stack(g("nf_p"), 1)
    nb_s = np.concatenate(g("nb_s"), 0)[None]
    nc_s = np.concatenate([a.reshape(NS, 2, D) for a in g("nc_s")], 0)[None]
    nf_s = np.concatenate([a.reshape(2, NS, 2, 2 * DFF) for a in g("nf_s")], 1)
    nv_s = np.concatenate([a.reshape(NS, TS_, D) for a in g("nv_s")], 0)[None]
    return (y_prompt, y_sample, nb_p, nc_p, nf_p, nb_s, nc_s, nf_s, nv_s)
```
